# Optimizing a Trainium2 kernel written in Bass

```python
import jax
import jax.numpy as jnp
from jax import lax
import numpy as np

D_MODEL = 2048
BATCH = 4
SEQ = 2048
DEPTH = 2
DEC_BATCH = 32
DEC_SEQ = 4
PAST_LEN = 8192
PAGE_SIZE = 128

N_BRANCH = 4
BRANCH_WIDTH = D_MODEL // N_BRANCH
ATT_WINDOWS = (128, 512, 2048)
ATT_DILATIONS = (1, 4, 16)
ATT_GROUPS = 3
ATT_HEADS_PER_GROUP = 4
ATT_HEADS = ATT_GROUPS * ATT_HEADS_PER_GROUP
ATT_HEAD_DIM = BRANCH_WIDTH // ATT_HEADS_PER_GROUP
ATT_WIDTH = ATT_HEADS * ATT_HEAD_DIM
ATT_Q_BLOCK = 128
REL_BUCKETS = 32
REL_MAX_DISTANCE = 2048
GMLP_WIDTH = BRANCH_WIDTH
GMLP_GROUPS = 4
GMLP_GROUP_WIDTH = GMLP_WIDTH // GMLP_GROUPS
GMLP_CHUNK = 128
POOL_WINDOWS = (2, 4, 8, 16)
POOL_WIDTH = BRANCH_WIDTH
POOL_GROUP_WIDTH = POOL_WIDTH // 4
POOL_STATE = 15
RET_HEADS = 4
RET_HEAD_DIM = BRANCH_WIDTH // RET_HEADS
RET_WIDTH = BRANCH_WIDTH
RET_CHUNK = 128
ROPE_BASE = 10000.0
D_FF = 5632
EPS = 1e-6
IN_SIZES = (ATT_WIDTH, ATT_WIDTH, ATT_WIDTH, GMLP_WIDTH, GMLP_WIDTH, POOL_WIDTH,
            RET_WIDTH, RET_WIDTH, RET_WIDTH, RET_WIDTH, N_BRANCH * D_MODEL)
IN_WIDTH = 16384

kernel_name = "hybrid_dilated_gmlp_pool_retention_decoder_step"


def rmsnorm(x, g):
    xf = x.astype(jnp.float32)
    y = xf * lax.rsqrt(jnp.mean(xf * xf, axis=-1, keepdims=True) + EPS)
    return (y * g.astype(jnp.float32)).astype(x.dtype)


def swiglu(x, w_gate, w_up, w_down):
    return (jax.nn.silu(x @ w_gate) * (x @ w_up)) @ w_down


def t5_causal_buckets(dist):
    max_exact = REL_BUCKETS // 2
    d = np.maximum(dist, 1).astype(np.float32)
    large = max_exact + (np.log(d / max_exact) / np.log(REL_MAX_DISTANCE / max_exact)
                         * (REL_BUCKETS - max_exact)).astype(np.int32)
    large = np.minimum(large, REL_BUCKETS - 1)
    return np.where(dist < max_exact, dist, large).astype(np.int32)


def dilated_group_attention(q, k_ext, v_ext, n_past, dil, bias):
    B, T, H, hd = q.shape
    nk = bias.shape[0]
    qb = T if T <= ATT_Q_BLOCK else ATT_Q_BLOCK
    nb = T // qb
    steps = dil * jnp.arange(nk)
    scale = hd ** -0.5
    bias_t = bias.astype(jnp.float32).T

    def block(bi):
        s0 = bi * qb
        q_blk = lax.dynamic_slice_in_dim(q, s0, qb, axis=1)
        idx = n_past + s0 + jnp.arange(qb)[:, None] - steps[None, :]
        valid = idx >= 0
        idx = jnp.maximum(idx, 0)
        k_g = k_ext[:, idx]
        v_g = v_ext[:, idx]
        s = jnp.einsum('bqhd,bqnhd->bqhn', q_blk, k_g).astype(jnp.float32) * scale + bias_t
        s = jnp.where(valid[None, :, None, :], s, -jnp.inf)
        m = jnp.max(s, axis=-1, keepdims=True)
        p = jnp.exp(s - m)
        den = jnp.sum(p, axis=-1)
        o = jnp.einsum('bqhn,bqnhd->bqhd', p, v_g.astype(jnp.float32)) / den[..., None]
        return o, m[..., 0] + jnp.log(den)

    o, lse = lax.map(block, jnp.arange(nb))
    o = jnp.moveaxis(o, 0, 1).reshape(B, T, H, hd)
    lse = jnp.moveaxis(lse, 0, 1).reshape(B, T, H)
    return o, lse


def pool_mixer(xc, buf, start, w_pool, pool_scale):
    B, T, C = xc.shape
    L = buf.shape[1]
    ext = jnp.concatenate([buf.astype(xc.dtype), xc], axis=1)
    csum = jnp.cumsum(ext.astype(jnp.float32), axis=1)
    csum = jnp.concatenate([jnp.zeros((B, 1, C), jnp.float32), csum], axis=1)
    end = L + 1 + jnp.arange(T)
    pos = start + jnp.arange(T)
    pooled = []
    for gi, win in enumerate(POOL_WINDOWS):
        sl = slice(gi * POOL_GROUP_WIDTH, (gi + 1) * POOL_GROUP_WIDTH)
        wsum = csum[:, end, sl] - csum[:, end - win, sl]
        cnt = jnp.minimum(pos + 1, win).astype(jnp.float32)
        pooled.append(wsum / cnt[None, :, None])
    diff = jnp.concatenate(pooled, axis=-1) - xc.astype(jnp.float32)
    diff = diff.reshape(B, T, len(POOL_WINDOWS), POOL_GROUP_WIDTH)
    y = jnp.einsum('btgc,gcd->btgd', diff, w_pool.astype(jnp.float32)).reshape(B, T, C)
    y = y * pool_scale.astype(jnp.float32)
    return y.astype(xc.dtype), ext[:, ext.shape[1] - L:]


def rotate(x, pos):
    d = x.shape[-1]
    half = d // 2
    inv = ROPE_BASE ** (-jnp.arange(half, dtype=jnp.float32) / half)
    ang = pos.astype(jnp.float32)[:, None] * inv[None, :]
    cos = jnp.cos(ang)[None, :, None, :]
    sin = jnp.sin(ang)[None, :, None, :]
    x1 = x[..., :half].astype(jnp.float32)
    x2 = x[..., half:].astype(jnp.float32)
    return jnp.concatenate([x1 * cos - x2 * sin, x1 * sin + x2 * cos], axis=-1)


def retention(q, k, v, s0):
    B, T, H, d = q.shape
    c = RET_CHUNK if T % RET_CHUNK == 0 else T
    n = T // c
    lg = jnp.log1p(-jnp.power(2.0, -5.0 - jnp.arange(H, dtype=jnp.float32)))
    i = jnp.arange(c, dtype=jnp.float32)
    diff = i[:, None] - i[None, :]
    inner_decay = jnp.where(diff >= 0, jnp.exp(jnp.maximum(diff, 0.0)[None] * lg[:, None, None]), 0.0)
    q_decay = jnp.exp((i + 1.0)[None, :] * lg[:, None])
    k_decay = jnp.exp((c - 1.0 - i)[None, :] * lg[:, None])
    chunk_decay = jnp.exp(c * lg)

    def to_chunks(a):
        return a.astype(jnp.float32).reshape(B, n, c, H, d).transpose(1, 0, 3, 2, 4)

    def step(S, inp):
        qi, ki, vi = inp
        att = jnp.einsum('bhtd,bhsd->bhts', qi, ki) * inner_decay
        o = (jnp.einsum('bhts,bhsd->bhtd', att, vi)
             + jnp.einsum('bhtd,bhde->bhte', qi, S) * q_decay[None, :, :, None])
        S = (S * chunk_decay[None, :, None, None]
             + jnp.einsum('bhsd,bhse->bhde', ki * k_decay[None, :, :, None], vi))
        return S, o

    S, o = lax.scan(step, s0.astype(jnp.float32), (to_chunks(q), to_chunks(k), to_chunks(v)))
    o = o.transpose(1, 0, 3, 2, 4).reshape(B, T, H, d)
    return o, S


def decoder_layer(x, start, kv_bufs, pool_buf, ret_state, rel_bias,
                  g_ffn1, w_ffn1_gate, w_ffn1_up, w_ffn1_down,
                  g_mix, w_in, g_gmlp, w_spatial, b_spatial, w_pool, pool_scale, g_ret,
                  w_branch, w_out, g_ffn2, w_ffn2_gate, w_ffn2_up, w_ffn2_down):
    B, T, _ = x.shape
    f32 = jnp.float32
    x = x + 0.5 * swiglu(rmsnorm(x, g_ffn1), w_ffn1_gate, w_ffn1_up, w_ffn1_down)
    h = rmsnorm(x, g_mix)
    z = h @ w_in
    splits = [int(s) for s in np.cumsum(IN_SIZES)[:-1]]
    a_q, a_k, a_v, b_u, b_v, c_in, d_q, d_k, d_v, d_g, z_gate = jnp.split(z, splits, axis=-1)

    hg, hd = ATT_HEADS_PER_GROUP, ATT_HEAD_DIM
    shp = (B, T, ATT_GROUPS, hg, hd)
    a_q, a_k, a_v = a_q.reshape(shp), a_k.reshape(shp), a_v.reshape(shp)
    outs, lses, kv_new = [], [], []
    for gi in range(ATT_GROUPS):
        win, dil = ATT_WINDOWS[gi], ATT_DILATIONS[gi]
        kv_g = jnp.stack([a_k[:, :, gi], a_v[:, :, gi]], axis=2)
        kv_new.append(kv_g)
        kv_ext = jnp.concatenate([kv_bufs[gi].astype(kv_g.dtype), kv_g], axis=1)
        n_key = win // dil + 1
        bucket = t5_causal_buckets(dil * np.arange(n_key))
        bias = rel_bias[bucket][:, gi * hg:(gi + 1) * hg]
        o, lse = dilated_group_attention(a_q[:, :, gi], kv_ext[:, :, 0], kv_ext[:, :, 1],
                                         kv_bufs[gi].shape[1], dil, bias)
        outs.append(o)
        lses.append(lse)
    w_den = jax.nn.softmax(jnp.stack(lses), axis=0)
    o_a = jnp.sum(w_den[..., None] * jnp.stack(outs), axis=0).reshape(B, T, BRANCH_WIDTH).astype(x.dtype)

    u = jax.nn.gelu(b_u)
    vn = rmsnorm(jax.nn.gelu(b_v), g_gmlp)
    c = T if T < GMLP_CHUNK else GMLP_CHUNK
    vc = vn.reshape(B, T // c, c, GMLP_GROUPS, GMLP_GROUP_WIDTH)
    ws = w_spatial[:, :c, :c] * jnp.tril(jnp.ones((c, c), w_spatial.dtype))
    mixed = jnp.einsum('gts,bnsgc->bntgc', ws, vc) + b_spatial[:, :c].T[None, None, :, :, None]
    o_b = u * mixed.reshape(B, T, GMLP_WIDTH)

    o_c, pool_new = pool_mixer(c_in, pool_buf, start, w_pool, pool_scale)

    hs = (B, T, RET_HEADS, RET_HEAD_DIM)
    pos = start + jnp.arange(T)
    rq = rotate(d_q.reshape(hs), pos)
    rk = rotate(d_k.reshape(hs), pos) * (RET_HEAD_DIM ** -0.5)
    o_d, ret_new = retention(rq, rk, d_v.reshape(hs), ret_state)
    o_d = o_d * lax.rsqrt(jnp.mean(o_d * o_d, axis=-1, keepdims=True) + EPS)
    o_d = (o_d.reshape(B, T, RET_WIDTH) * g_ret.astype(f32) * jax.nn.silu(d_g.astype(f32))).astype(x.dtype)

    br = jnp.stack([o_a, o_b.astype(x.dtype), o_c, o_d], axis=2)
    proj = jnp.einsum('btnc,ncd->btnd', br, w_branch)
    gate = jax.nn.sigmoid(z_gate.reshape(B, T, N_BRANCH, D_MODEL))
    x = x + jnp.sum(gate * proj, axis=2) @ w_out
    x = x + 0.5 * swiglu(rmsnorm(x, g_ffn2), w_ffn2_gate, w_ffn2_up, w_ffn2_down)
    return x, kv_new, pool_new, ret_new, vn


def setup_inputs(seed: int = 0) -> dict:
    key = jax.random.key(seed)
    keys = jax.random.split(key, 32)
    f32 = jnp.float32
    hg, hd = ATT_HEADS_PER_GROUP, ATT_HEAD_DIM

    def nrm(i, shape, scale):
        return jax.random.normal(keys[i], shape, f32) * scale

    def gain(i, shape):
        return 1.0 + 0.05 * jax.random.normal(keys[i], shape, f32)

    return {
        "x_prompt": nrm(0, (BATCH, SEQ, D_MODEL), 1.0),
        "x_sample": nrm(1, (DEC_BATCH, DEC_SEQ, D_MODEL), 1.0),
        "cache_attn_kv_w128": nrm(2, (DEPTH, DEC_BATCH, min(ATT_WINDOWS[0], PAST_LEN), 2, hg, hd), 1.0),
        "cache_attn_kv_w512": nrm(3, (DEPTH, DEC_BATCH, min(ATT_WINDOWS[1], PAST_LEN), 2, hg, hd), 1.0),
        "cache_attn_kv_w2048": nrm(4, (DEPTH, DEC_BATCH, min(ATT_WINDOWS[2], PAST_LEN), 2, hg, hd), 1.0),
        "state_pool": nrm(5, (DEPTH, DEC_BATCH, POOL_STATE, POOL_WIDTH), 1.0),
        "state_ret": nrm(6, (DEPTH, DEC_BATCH, RET_HEADS, RET_HEAD_DIM, RET_HEAD_DIM), 0.3),
        "rel_bias": nrm(7, (REL_BUCKETS, ATT_HEADS), 0.5),
        "g_ffn1": gain(8, (DEPTH, D_MODEL)),
        "w_ffn1_gate": nrm(9, (DEPTH, D_MODEL, D_FF), D_MODEL ** -0.5),
        "w_ffn1_up": nrm(10, (DEPTH, D_MODEL, D_FF), D_MODEL ** -0.5),
        "w_ffn1_down": nrm(11, (DEPTH, D_FF, D_MODEL), D_FF ** -0.5),
        "g_mix": gain(12, (DEPTH, D_MODEL)),
        "w_in": nrm(13, (DEPTH, D_MODEL, IN_WIDTH), D_MODEL ** -0.5),
        "g_gmlp": gain(14, (DEPTH, GMLP_WIDTH)),
        "w_spatial": nrm(15, (DEPTH, GMLP_GROUPS, GMLP_CHUNK, GMLP_CHUNK), GMLP_CHUNK ** -0.5),
        "b_spatial": 1.0 + nrm(16, (DEPTH, GMLP_GROUPS, GMLP_CHUNK), 0.02),
        "w_pool": nrm(17, (DEPTH, len(POOL_WINDOWS), POOL_GROUP_WIDTH, POOL_GROUP_WIDTH), POOL_GROUP_WIDTH ** -0.5),
        "pool_scale": gain(18, (DEPTH, POOL_WIDTH)),
        "g_ret": gain(19, (DEPTH, RET_WIDTH)),
        "w_branch": nrm(20, (DEPTH, N_BRANCH, BRANCH_WIDTH, D_MODEL), BRANCH_WIDTH ** -0.5),
        "w_out": nrm(21, (DEPTH, D_MODEL, D_MODEL), D_MODEL ** -0.5),
        "g_ffn2": gain(22, (DEPTH, D_MODEL)),
        "w_ffn2_gate": nrm(23, (DEPTH, D_MODEL, D_FF), D_MODEL ** -0.5),
        "w_ffn2_up": nrm(24, (DEPTH, D_MODEL, D_FF), D_MODEL ** -0.5),
        "w_ffn2_down": nrm(25, (DEPTH, D_FF, D_MODEL), D_FF ** -0.5),
        "g_final": gain(26, (D_MODEL,)),
    }


def reference(x_prompt, x_sample, cache_attn_kv_w128, cache_attn_kv_w512, cache_attn_kv_w2048,
              state_pool, state_ret, rel_bias,
              g_ffn1, w_ffn1_gate, w_ffn1_up, w_ffn1_down,
              g_mix, w_in, g_gmlp, w_spatial, b_spatial, w_pool, pool_scale, g_ret,
              w_branch, w_out, g_ffn2, w_ffn2_gate, w_ffn2_up, w_ffn2_down, g_final):
    caches = (cache_attn_kv_w128, cache_attn_kv_w512, cache_attn_kv_w2048)
    bp, tp = x_prompt.shape[0], x_prompt.shape[1]
    hp, hs = x_prompt, x_sample
    kv_p = [[], [], []]
    kv_s = [[], [], []]
    pool_p, pool_s, ret_p, ret_s, gv_s = [], [], [], [], []
    for l in range(DEPTH):
        lw = (rel_bias, g_ffn1[l], w_ffn1_gate[l], w_ffn1_up[l], w_ffn1_down[l],
              g_mix[l], w_in[l], g_gmlp[l], w_spatial[l], b_spatial[l], w_pool[l], pool_scale[l], g_ret[l],
              w_branch[l], w_out[l], g_ffn2[l], w_ffn2_gate[l], w_ffn2_up[l], w_ffn2_down[l])
        empty_kv = tuple(jnp.zeros((bp, 0, 2, ATT_HEADS_PER_GROUP, ATT_HEAD_DIM), hp.dtype)
                         for _ in ATT_WINDOWS)
        hp, kvn, pooln, retn, _ = decoder_layer(
            hp, 0, empty_kv, jnp.zeros((bp, POOL_STATE, POOL_WIDTH), hp.dtype),
            jnp.zeros((bp, RET_HEADS, RET_HEAD_DIM, RET_HEAD_DIM), jnp.float32), *lw)
        for gi in range(ATT_GROUPS):
            keep = min(ATT_WINDOWS[gi], tp)
            kv_p[gi].append(kvn[gi][:, tp - keep:])
        pool_p.append(pooln)
        ret_p.append(retn)
        hs, kvn, pooln, retn, gvn = decoder_layer(
            hs, PAST_LEN, tuple(cc[l] for cc in caches), state_pool[l], state_ret[l], *lw)
        for gi in range(ATT_GROUPS):
            kv_s[gi].append(kvn[gi])
        pool_s.append(pooln)
        ret_s.append(retn)
        gv_s.append(gvn)
    y_prompt = rmsnorm(hp, g_final)
    y_sample = rmsnorm(hs, g_final)
    return (y_prompt, y_sample,
            jnp.stack(kv_p[0]), jnp.stack(kv_p[1]), jnp.stack(kv_p[2]),
            jnp.stack(kv_s[0]), jnp.stack(kv_s[1]), jnp.stack(kv_s[2]),
            jnp.stack(pool_p), jnp.stack(pool_s),
            jnp.stack(ret_p), jnp.stack(ret_s),
            jnp.stack(gv_s))
```

```python
import numpy as np
import concourse.bass as bass
import concourse.mybir as mybir
from concourse.bass_utils import run_bass_kernel_spmd

F32 = mybir.dt.float32
BF16 = mybir.dt.bfloat16
ALU = mybir.AluOpType
AF = mybir.ActivationFunctionType
AX = mybir.AxisListType


class Buf:
    __slots__ = ("name", "w", "r", "ps")

    def __init__(self, name, ps=False):
        self.name = name
        self.ps = ps
        self.w = None
        self.r = {}


class Sched:
    SIM = False

    def __init__(self, nc, n_dma_sems=40):
        self.nc = nc
        self.eng = {"pe": nc.tensor, "act": nc.scalar, "dve": nc.vector,
                    "pool": nc.gpsimd, "sp": nc.sync}
        self.sems = {}
        self.cnt = {}
        for e in self.eng:
            self.sems[e] = nc.alloc_semaphore("s_" + e)
            self.cnt[e] = 0
        self.dsem = [nc.alloc_semaphore("d%d" % i) for i in range(n_dma_sems)]
        for i in range(n_dma_sems):
            self.sems["d%d" % i] = self.dsem[i]
            self.cnt["d%d" % i] = 0
        self.dnext = 0
        self.seen = {e: {} for e in self.eng}
        self.ninst = 0
        self.last_drain = {}
        self.sw = [nc.alloc_semaphore("sw%d" % i) for i in range(24)]
        for i in range(24):
            self.sems["sw%d" % i] = self.sw[i]
            self.cnt["sw%d" % i] = 0
        self.swn = 0
        self.swfifo = []
        self.pool_issued = 0

    def _relay(self):
        j = self.swfifo.pop(0)
        p = self.eng["pool"]
        p.wait_ge(self.sw[j], 16)
        p.sem_inc(self.sw[j], -16)
        p.sem_inc(self.sems["pool"], 1)
        self.cnt["pool"] += 1
        self.ninst += 3

    def flush_pool(self):
        while self.swfifo:
            self._relay()

    def dma_pool(self, out, in_, reads=(), writes=(), wclass=False):
        j = self.swn
        self.swn = (j + 1) % len(self.sw)
        key = "sw%d" % j
        self._wait("pool", key, self.cnt[key])
        self._deps("pool", reads, writes)
        ins = self.eng["pool"].dma_start(out=out, in_=in_)
        self.cnt[key] += 16
        ins.then_inc(self.sw[j], 16)
        self._mark((key, self.cnt[key]), reads, writes)
        self.ninst += 1
        return ins

    def _wait(self, e, key, val):
        if val <= 0:
            return
        if self.seen[e].get(key, 0) >= val:
            return
        self.seen[e][key] = val
        self.eng[e].wait_ge(self.sems[key], val)
        self.ninst += 1

    def _deps(self, e, reads, writes):
        need = {}

        def add(t):
            if t is None:
                return
            k, v = t
            if need.get(k, 0) < v:
                need[k] = v
        for b in reads:
            add(b.w)
            if b.ps:
                for k, v in b.r.items():
                    if k != e:
                        add((k, v))
        for b in writes:
            add(b.w)
            for k, v in b.r.items():
                add((k, v))
        for k, v in need.items():
            if k == e and (e == "pe" or (e != "dve" and not Sched.SIM)):
                continue
            self._wait(e, k, v)

    def _mark(self, tick, reads, writes):
        k, v = tick
        for b in reads:
            if b.r.get(k, 0) < v:
                b.r[k] = v
        for b in writes:
            b.w = tick
            b.r = {}

    def op(self, e, fn, reads=(), writes=()):
        self._deps(e, reads, writes)
        ins = fn(self.eng[e])
        self.cnt[e] += 1
        ins.then_inc(self.sems[e], 1)
        self._mark((e, self.cnt[e]), reads, writes)
        self.ninst += 1
        return ins

    def dma(self, q, out, in_, reads=(), writes=(), **kw):
        if q == "pool":
            return self.dma_pool(out, in_, reads, writes, wclass=kw.get("wclass", False))
        i = self.dnext
        self.dnext = (self.dnext + 1) % len(self.dsem)
        key = "d%d" % i
        self._wait(q, key, self.cnt[key])
        self._deps(q, reads, writes)
        ins = self.eng[q].dma_start(out=out, in_=in_, **kw)
        self.cnt[key] += 16
        ins.then_inc(self.dsem[i], 16)
        self._mark((key, self.cnt[key]), reads, writes)
        self.ninst += 1
        return ins

    def finish(self, bufs):
        for b in bufs:
            if b.w is not None:
                self._wait("sp", b.w[0], b.w[1])


D = 2048
DFF = 5632
NL = 2
SEQ = 2048
TT = 512
NSEQ_S = 4
TS = 16
WINS = (128, 512, 2048)
DILS = (1, 4, 16)
EPS = 1e-6
NEG = -30000.0
INW = 16384 + 1024
C_AQ, C_AK, C_AV, C_BU, C_BV, C_C, C_DQ, C_DK, C_DV, C_DG, C_GATE, C_DQS, C_DKS = (
    0, 1536, 3072, 4608, 5120, 5632, 6144, 6656, 7168, 7680, 8192, 16384, 16896)
TBL_OFF = (0, 256, 896)
TBL_W = 3072
ATT_SCALE = 128 ** -0.5
NWSLOT = 3


def _t5_buckets(dist):
    max_exact = 16
    d = np.maximum(dist, 1).astype(np.float32)
    large = max_exact + (np.log(d / max_exact) / np.log(2048 / max_exact) * (32 - max_exact)).astype(np.int32)
    large = np.minimum(large, 31)
    return np.where(dist < max_exact, dist, large).astype(np.int32)


def host_tables(rel_bias):
    tbl = np.full((4, 128, TBL_W), NEG, np.float32)
    q = np.arange(128)[:, None]
    for g in range(3):
        win, dil = WINS[g], DILS[g]
        u = np.arange(win + 128)[None, :]
        d = q + win - u
        valid = (d >= 0) & (d <= win) & (d % dil == 0)
        bk = _t5_buckets(np.clip(d, 0, None))
        for h in range(4):
            vals = rel_bias[bk, 4 * g + h]
            tbl[h, :, TBL_OFF[g]:TBL_OFF[g] + win + 128] = np.where(valid, vals, NEG)
    return tbl


def host_consts():
    c = {}
    half = 64
    inv = (10000.0 ** (-np.arange(half, dtype=np.float32) / half)).astype(np.float32)
    cc = np.zeros((5, 128, TT), np.float32)
    ss = np.zeros((5, 128, TT), np.float32)
    for p in range(5):
        if p < 4:
            pos = (TT * p + np.arange(TT)).astype(np.float32)
        else:
            pos = np.zeros(TT, np.float32)
            pos[:TS] = 8192 + (np.arange(TS) % 4)
        ang = pos[None, :] * np.concatenate([inv, inv])[:, None]
        cc[p] = np.cos(ang)
        sn = np.sin(ang)
        ss[p, :64] = -sn[:64]
        ss[p, 64:] = sn[64:]
    c["rot_cc"] = cc
    c["rot_ss"] = ss
    lg = np.log1p(-np.power(2.0, -5.0 - np.arange(4, dtype=np.float32))).astype(np.float32)
    for nm, cs in (("p", 128), ("s", 4)):
        i = np.arange(cs, dtype=np.float32)
        diff = i[:, None] - i[None, :]
        inner = np.where(diff >= 0, np.exp(np.maximum(diff, 0.0)[None] * lg[:, None, None]), 0.0)
        c["ret_dt_" + nm] = np.ascontiguousarray(inner.transpose(2, 0, 1)).astype(np.float32)
        qd = np.exp((i + 1.0)[None, :] * lg[:, None]).astype(np.float32)
        c["ret_qd_" + nm] = np.ascontiguousarray(np.broadcast_to(qd[None], (128, 4, cs))).astype(np.float32)
        kd = np.exp((cs - 1.0 - i)[None, :] * lg[:, None]).astype(np.float32)
        c["ret_kd_" + nm] = np.ascontiguousarray(kd.T).astype(np.float32)
        c["ret_cd_" + nm] = [float(np.exp(np.float32(cs) * lg[h])) for h in range(4)]
    ic = np.zeros((5, 128, 4, TT), np.float32)
    for p in range(5):
        pos = (TT * p + np.arange(TT)) if p < 4 else np.full(TT, 8192)
        for g, win in enumerate((2, 4, 8, 16)):
            ic[p, :, g, :] = (1.0 / np.minimum(pos + 1, win).astype(np.float32))[None, :]
    c["pool_ic"] = ic
    sw = np.zeros((128, 128), np.float32)
    c["ident"] = np.eye(128, dtype=np.float32)
    return c


def _prod(s):
    n = 1
    for v in s:
        n *= int(v)
    return n


class Prog:
    def __init__(self, passes=(0, 1, 2, 3, 4), nlayers=NL):
        self.nc = nc = bass.Bass("TRN2", target_bir_lowering=False)
        self.S = Sched(nc)
        self.passes = passes
        self.nlayers = nlayers
        self.cst = host_consts()
        self._dram()
        self._sbuf()
        self._emit()

    def _dram(self):
        nc = self.nc

        self.input_specs = {}

        def I(name, shape, dt=F32):
            self.input_specs[name] = tuple(shape)
            return nc.dram_tensor(name, list(shape), dt, kind="ExternalInput").ap()

        def O(name, shape):
            return nc.dram_tensor(name, list(shape), F32, kind="ExternalOutput").ap()

        self.xp = I("xp", [SEQ, D]); self.xs = I("xs", [TS, D])
        self.ckv = [I("ckv%d" % g, [NL, NSEQ_S, WINS[g], 2, 4, 128]) for g in range(3)]
        self.spool = I("spool", [NL, NSEQ_S, 15, 512]); self.sret = I("sret", [NL, NSEQ_S, 4, 128, 128])
        self.w_g1 = I("w_ffn1_gate", [NL, D, DFF]); self.w_u1 = I("w_ffn1_up", [NL, D, DFF]); self.w_d1 = I("w_ffn1_down", [NL, DFF, D])
        self.w_g2 = I("w_ffn2_gate", [NL, D, DFF]); self.w_u2 = I("w_ffn2_up", [NL, D, DFF]); self.w_d2 = I("w_ffn2_down", [NL, DFF, D])
        self.w_in = I("w_in", [NL, D, INW]); self.w_br = I("w_branch", [NL, 4 * 512, D]); self.w_out = I("w_out", [NL, D, D])
        self.gvec = I("gvec", [128, 7, 16])
        self.gsm = I("gsm", [128, NL, 2, 4])
        self.ggm = I("ggm", [NL, 512]); self.bsp = I("bsp", [NL, 512]); self.bsp_s = I("bsp_s", [NL, 4 * TS])
        self.wsT = I("wsT", [NL, 128, 4, 128]); self.wsT_s = I("wsT_s", [NL, TS, 4, TS])
        self.wpool = I("wpool", [NL, 128, 4, 128])
        self.tbl = I("tbl", [4, 128, TBL_W])
        self.rot_cc = I("rot_cc", [5, 128, TT]); self.rot_ss = I("rot_ss", [5, 128, TT])
        self.dt_p = I("ret_dt_p", [128, 4, 128]); self.qd_p = I("ret_qd_p", [128, 4, 128]); self.kd_p = I("ret_kd_p", [128, 4])
        self.dt_s = I("ret_dt_s", [4, 4, 4]); self.qd_s = I("ret_qd_s", [128, 4, 4]); self.kd_s = I("ret_kd_s", [4, 4])
        self.pool_ic = I("pool_ic", [5, 128, 4, TT]); self.ident_d = I("ident", [128, 128])
        self.y_p = O("y_p", [SEQ, D]); self.y_s = O("y_s", [TS, D])
        self.kv_p = [O("kv_p%d" % g, [NL, min(WINS[g], SEQ), 2, 4, 128]) for g in range(3)]
        self.kv_s = [O("kv_s%d" % g, [NL, NSEQ_S, 4, 2, 4, 128]) for g in range(3)]
        self.pool_p = O("pool_p", [NL, 15, 512]); self.pool_s = O("pool_s", [NL, NSEQ_S, 15, 512])
        self.ret_p = O("ret_p", [NL, 4, 128, 128]); self.ret_s = O("ret_s", [NL, NSEQ_S, 4, 128, 128])
        self.gv_s = O("gv_s", [NL, NSEQ_S, 4, 512])
        self.kT_hist = nc.dram_tensor("kT_hist", [NL, 3, 4, 128, SEQ], BF16, kind="Internal").ap()
        self.v_hist = nc.dram_tensor("v_hist", [NL, 3, SEQ, 512], BF16, kind="Internal").ap()
        NWT = 300
        _scr = [nc.dram_tensor("wscr%d" % i, [100, 128, 8192], BF16, kind="Internal").ap() for i in range(NWT // 100)]
        self.wscr = [_scr[t // 100][t % 100] for t in range(NWT)]
        self.b_wscr = [Buf("wscr%d" % i) for i in range(NWT)]
        self.wmode = None
        self.wtile = 0
        self.b_kth = [[Buf("kth%d%d" % (l, g)) for g in range(3)] for l in range(NL)]
        self.b_vh = [[Buf("vh%d%d" % (l, g)) for g in range(3)] for l in range(NL)]

    def _sbuf(self):
        nc = self.nc
        NB = 204 * 1024
        self.arena = nc.alloc_sbuf_tensor("arena", [128, NB // 2], BF16)
        self.top = 0
        self.NB = NB

        def P(shape, dt):
            ap = self.view(self.top, shape, dt)
            self.top += _prod(shape) * (4 if dt == F32 else 2)
            self.top = (self.top + 63) // 64 * 64
            return ap
        self.xT = P([16, TT], F32); self.b_x = [Buf("x%d" % c) for c in range(16)]
        self.hT = P([16, TT], BF16); self.b_h = [Buf("h%d" % c) for c in range(16)]
        self.wslot = [P([8192], BF16) for _ in range(NWSLOT)]; self.b_w = [[Buf("w%d_%d" % (i, n)) for n in range(4)] for i in range(NWSLOT)]
        self.wnext = 0
        self.QT = P([12, TT], BF16); self.b_q = [Buf("q%d" % i) for i in range(12)]
        self.brT = P([16, TT], BF16); self.b_br = [Buf("br%d" % i) for i in range(16)]
        self.ident = P([128], F32); self.identb = P([128], BF16); self.onesb = P([128], BF16)
        self.b_const = Buf("const")
        self.gv = P([7, 16], F32); self.gs = P([NL, 2, 4], F32)
        self.ggm_b = P([NL, 512], F32); self.bsp_b = P([NL, 512], F32); self.bsps_b = P([NL, 4 * TS], F32)
        self.rstd = P([TT], F32); self.b_rstd = Buf("rstd")
        self.sq = [P([TT], BF16) for _ in range(2)]; self.b_sq = [Buf("sq0"), Buf("sq1")]
        self.sml = P([16], F32); self.b_sml = Buf("sml"); self.b_sml2 = Buf("sml2")
        self.Sst = P([NL, 4, 128], F32); self.Sbf = P([NL, 4, 128], BF16)
        self.b_S = [[Buf("S%d%d" % (l, h)) for h in range(4)] for l in range(NL)]
        self.b_Sb = [[Buf("Sb%d%d" % (l, h)) for h in range(4)] for l in range(NL)]
        self.halo = P([NL, 4, 15], F32); self.b_halo = [Buf("halo%d" % l) for l in range(NL)]
        self.dtp = P([4, 128], F32); self.qdp = P([4, 128], F32); self.kdp = P([4], F32)
        self.dts = P([4, 4], F32); self.qds = P([4, 4], F32); self.kds = P([4], F32)
        self.epsb = P([1], F32)
        self.ubase = self.top
        self.usize = NB - self.top
        print("SBUF persistent bytes", self.top, "U bytes", self.usize)
        self.ubufs = []
        self.ustate = {}
        self.utop = 0
        self.ps = [nc.alloc_psum_tensor("ps%d" % i, [128, 512], F32) for i in range(6)]
        self.b_ps = [Buf("ps%d" % i, ps=True) for i in range(6)]
        self.psn = 0
        self.pst = [nc.alloc_psum_tensor("pst%d" % i, [128, 1024], BF16) for i in range(2)]
        self.b_pst = [Buf("pst%d" % i, ps=True) for i in range(2)]
        self.pstn = 0

    def view(self, off, shape, dt):
        esz = 4 if dt == F32 else 2
        n = _prod(shape)
        assert off % 4 == 0 and off + n * esz <= self.NB, (off, shape, self.NB)
        ap = self.arena[:, off // 2: off // 2 + n * esz // 2]
        if dt == F32:
            ap = ap.bitcast(F32)
        if len(shape) == 2:
            ap = ap.rearrange("p (a b) -> p a b", b=shape[1])
        elif len(shape) == 3:
            ap = ap.rearrange("p (a b c) -> p a b c", b=shape[1], c=shape[2])
        return ap

    def u_phase(self):
        st = dict(self.ustate)
        for b in self.ubufs:
            if b.w is not None and st.get(b.w[0], 0) < b.w[1]:
                st[b.w[0]] = b.w[1]
            for k, v in b.r.items():
                if st.get(k, 0) < v:
                    st[k] = v
        self.ustate = st
        self.ubufs = []
        self.utop = 0

    def U(self, shape, dt, name="u"):
        ap = self.view(self.ubase + self.utop, shape, dt)
        self.utop += _prod(shape) * (4 if dt == F32 else 2)
        self.utop = (self.utop + 63) // 64 * 64
        assert self.utop <= self.usize, ("U overflow", self.utop, self.usize)
        b = Buf(name)
        b.r = dict(self.ustate)
        self.ubufs.append(b)
        return ap, b

    def getps(self):
        i = self.psn
        self.psn = (i + 1) % len(self.ps)
        return self.ps[i], self.b_ps[i]

    def getpst(self):
        i = self.pstn
        self.pstn = (i + 1) % len(self.pst)
        return self.pst[i], self.b_pst[i]

    def wload(self, src):
        i = self.wnext
        self.wnext = (i + 1) % NWSLOT
        shp = list(src.shape)
        n = _prod(shp[1:])
        assert n <= 8192
        flat = self.wslot[i][:, :n]
        dst = flat
        if len(shp) == 3:
            dst = dst.rearrange("p (a b) -> p a b", b=shp[2])
        elif len(shp) == 4:
            dst = dst.rearrange("p (a b c) -> p a b c", b=shp[2], c=shp[3])
        t = self.wtile
        self.wtile += 1
        if self.wmode == "load":
            self.S.dma("sp", flat, self.wscr[t][:, :n], reads=[self.b_wscr[t]], writes=self.b_w[i])
            return dst, self.b_w[i]
        if len(shp) == 4:
            for n_ in range(shp[2]):
                self.S.dma("pool", dst[:, :, n_, :], src[:, :, n_, :], writes=[self.b_w[i][n_]], wclass=True)
        else:
            self.S.dma("pool", dst, src, writes=self.b_w[i], wclass=True)
        if self.wmode == "save":
            self.S.dma("sp", self.wscr[t][:, :n], flat, reads=self.b_w[i], writes=[self.b_wscr[t]])
        return dst, self.b_w[i]

    def gemm(self, W, c0, ncols, KC, srcs, T, fm=None, tm=None, tgroups=(), ci0=0):
        S = self.S
        w, wb = self.wload(W[:, c0:c0 + ncols].rearrange("(kc p) n -> p kc n", p=128))
        if fm is not None:
            for cc in range(ncols // 128):
                ps, pb = self.getps()
                for kc in range(KC):
                    sa, sb = srcs[kc]
                    S.op("pe", lambda e, kc=kc, sa=sa, cc=cc, ps=ps: e.matmul(
                        ps[:, :T], w[:, kc, cc * 128:(cc + 1) * 128], sa[:, :T], start=(kc == 0), stop=(kc == KC - 1)),
                        reads=wb + [sb], writes=[pb])
                fm(ci0 + cc, ps[:, :T], pb)
        if tm is not None:
            for gi, (t0, M) in enumerate(tgroups):
                ps, pb = self.getps()
                for kc in range(KC):
                    sa, sb = srcs[kc]
                    S.op("pe", lambda e, kc=kc, sa=sa, ps=ps, t0=t0, M=M: e.matmul(
                        ps[:M, :ncols], sa[:, t0:t0 + M], w[:, kc, :], start=(kc == 0), stop=(kc == KC - 1)),
                        reads=wb + [sb], writes=[pb])
                tm(gi, ps[:M, :ncols], pb)

    def rmsnorm_fm(self, xs, T, gcol, outs, dim):
        S = self.S
        ps, pb = self.getps()
        n = len(xs)
        for c, (xa, xb) in enumerate(xs):
            sq, sqb = self.sq[c % 2], self.b_sq[c % 2]
            S.op("act", lambda e, xa=xa, sq=sq: e.activation(sq[:, :T], xa, AF.Square), reads=[xb], writes=[sqb])
            S.op("pe", lambda e, sq=sq, c=c: e.matmul(ps[:, :T], self.onesb[:, :], sq[:, :T], start=(c == 0), stop=(c == n - 1)),
                 reads=[sqb, self.b_const], writes=[pb])
        S.op("act", lambda e: e.activation(self.rstd[:, :T], ps[:, :T], AF.Sqrt, bias=self.epsb[:, 0:1], scale=1.0 / dim),
             reads=[pb, self.b_const], writes=[self.b_rstd])
        S.op("dve", lambda e: e.reciprocal(self.rstd[:, :T], self.rstd[:, :T]), reads=[self.b_rstd], writes=[self.b_rstd])
        for c, (xa, xb) in enumerate(xs):
            oa, ob = outs[c]
            S.op("dve", lambda e, xa=xa, oa=oa, c=c: e.scalar_tensor_tensor(
                oa, xa, gcol(c), self.rstd[:, :T], ALU.mult, ALU.mult), reads=[xb, self.b_rstd, self.b_const], writes=[ob])

    def ffn(self, l, which, T):
        S = self.S
        Wg, Wu, Wd = ((self.w_g1, self.w_u1, self.w_d1), (self.w_g2, self.w_u2, self.w_d2))[which]
        gi = (0 if which == 0 else 4) + l
        xs = [(self.xT[:, c, :T], self.b_x[c]) for c in range(16)]
        hs = [(self.hT[:, c, :T], self.b_h[c]) for c in range(16)]
        self.rmsnorm_fm(xs, T, lambda c: self.gv[:, gi, c:c + 1], hs, D)
        self.u_phase()
        hid, _ = self.U([44, TT], BF16, "hid")
        b_hid = [Buf("hid%d" % j) for j in range(44)]
        for b in b_hid:
            b.r = dict(self.ustate)
        self.ubufs.extend(b_hid)
        hsrc = [(self.hT[:, c, :], self.b_h[c]) for c in range(16)]
        for j0 in range(0, DFF, 512):
            def fg(ci, ps, pb):
                S.op("act", lambda e: e.activation(hid[:, ci, :T], ps, AF.Silu), reads=[pb], writes=[b_hid[ci]])

            def fu(ci, ps, pb):
                S.op("dve", lambda e: e.tensor_tensor(hid[:, ci, :T], hid[:, ci, :T], ps, ALU.mult), reads=[pb, b_hid[ci]], writes=[b_hid[ci]])
            self.gemm(Wg[l], j0, 512, 16, hsrc, T, fm=fg, ci0=j0 // 128)
            self.gemm(Wu[l], j0, 512, 16, hsrc, T, fm=fu, ci0=j0 // 128)
        hidsrc = [(hid[:, j, :], b_hid[j]) for j in range(44)]
        for d in range(16):
            def fd(ci, ps, pb):
                S.op("dve", lambda e: e.scalar_tensor_tensor(self.xT[:, ci, :T], ps, 0.5, self.xT[:, ci, :T], ALU.mult, ALU.add),
                     reads=[pb, self.b_x[ci]], writes=[self.b_x[ci]])
            self.gemm(Wd[l], d * 128, 128, 44, hidsrc, T, fm=fd, ci0=d)

    def setup_consts(self):
        S = self.S
        b = self.b_const
        S.dma("sp", self.ident[:, :], self.ident_d, writes=[b])
        S.op("dve", lambda e: e.memset(self.onesb[:, :], 1.0), writes=[b])
        S.op("dve", lambda e: e.memset(self.epsb[:, :], EPS), writes=[b])
        S.op("dve", lambda e: e.tensor_copy(self.identb[:, :], self.ident[:, :]), reads=[b], writes=[b])
        zt, zb = self.U([1024], BF16, "zt")
        S.op("dve", lambda e: e.memset(zt[:, :], 0.0), writes=[zb])
        for i in range(len(self.pst)):
            for hh in range(8):
                S.op("pe", lambda e, i=i, hh=hh: e.transpose(self.pst[i][:, hh * 128:(hh + 1) * 128], zt[:, 0:128], self.identb[:, :]),
                     reads=[zb, b], writes=[self.b_pst[i]])
        S.dma("sp", self.gv[:, :, :], self.gvec, writes=[b])
        S.dma("sp", self.gs[:, :, :, :], self.gsm, writes=[b])
        for l in range(NL):
            S.dma("sp", self.ggm_b[:, l, :], self.ggm[l:l + 1, :].partition_broadcast(128), writes=[b])
            S.dma("sp", self.bsp_b[:, l, :], self.bsp[l:l + 1, :].partition_broadcast(128), writes=[b])
            S.dma("sp", self.bsps_b[:, l, :], self.bsp_s[l:l + 1, :].partition_broadcast(128), writes=[b])
        S.dma("sp", self.dtp[:, :, :], self.dt_p, writes=[b]); S.dma("sp", self.qdp[:, :, :], self.qd_p, writes=[b])
        S.dma("sp", self.kdp[:, :], self.kd_p, writes=[b])
        S.dma("sp", self.dts[:4, :, :4], self.dt_s, writes=[b]); S.dma("sp", self.qds[:, :, :4], self.qd_s, writes=[b])
        S.dma("sp", self.kds[:4, :], self.kd_s, writes=[b])

    def load_x(self, kind, T0, T):
        S = self.S
        self.u_phase()
        groups = [(cc * 128, 128) for cc in range(4)] if kind == "p" else [(0, TS)]
        for (t0, M) in groups:
            xin, xb = self.U([D], F32, "xin")
            src = self.xp[T0 + t0:T0 + t0 + M, :] if kind == "p" else self.xs
            S.dma("sp", xin[:M, :], src, writes=[xb])
            for c0 in range(0, 16, 4):
                ps, pb = self.getps()
                for j in range(4):
                    c = c0 + j
                    S.op("pe", lambda e, c=c, j=j, ps=ps, M=M, xin=xin: e.transpose(ps[:, j * 128:j * 128 + M], xin[:M, c * 128:(c + 1) * 128], self.ident[:M, :M]),
                         reads=[xb, self.b_const], writes=[pb])
                S.op("act", lambda e, c0=c0, ps=ps, t0=t0, M=M: e.copy(
                    self.xT[:, c0:c0 + 4, t0:t0 + M], ps[:, :].rearrange("p (j t) -> p j t", t=128)[:, :, :M]),
                    reads=[pb], writes=[self.b_x[c0 + j] for j in range(4)])

    def final_out(self, kind, T0, T):
        S = self.S
        xs = [(self.xT[:, c, :T], self.b_x[c]) for c in range(16)]
        self.u_phase()
        yT, b_y = self.xT, self.b_x
        outs = xs
        yos = [self.U([D], F32, "yo") for _ in range(2)]
        self.rmsnorm_fm(xs, T, lambda c: self.gv[:, 6, c:c + 1], outs, D)
        groups = [(cc * 128, 128) for cc in range(4)] if kind == "p" else [(0, TS)]
        for gi_, (t0, M) in enumerate(groups):
            yo, yob = yos[gi_ % 2]
            for c0 in range(0, 16, 4):
                ps, pb = self.getps()
                for j in range(4):
                    c = c0 + j
                    S.op("pe", lambda e, c=c, j=j, ps=ps, t0=t0, M=M: e.transpose(ps[:M, j * 128:(j + 1) * 128], yT[:, c, t0:t0 + M], self.ident[:, :]),
                         reads=[b_y[c], self.b_const], writes=[pb])
                S.op("act", lambda e, c0=c0, ps=ps, yo=yo, M=M: e.copy(yo[:M, c0 * 128:(c0 + 4) * 128], ps[:M, :]), reads=[pb], writes=[yob])
            dst = self.y_p[T0 + t0:T0 + t0 + M, :] if kind == "p" else self.y_s
            S.dma("sp", dst, yo[:M, :], reads=[yob])

    def _emit(self):
        S = self.S
        self.setup_consts()
        prm = [p for p in self.passes if p < 4]
        save_pass = max(prm) if (prm and 4 in self.passes and getattr(Prog, "WSCR", True)) else None
        for p in self.passes:
            self.wtile = 0
            self.wmode = "save" if p == save_pass else ("load" if (p == 4 and save_pass is not None) else None)
            kind = "p" if p < 4 else "s"
            T0 = TT * p if p < 4 else 0
            T = TT if p < 4 else TS
            self.load_x(kind, T0, T)
            for l in range(self.nlayers):
                self.ffn(l, 0, T)
                if not getattr(self, "skip_mix", False):
                    self.mixers(l, p, kind, T0, T)
                if not getattr(self, "skip_ffn2", False):
                    self.ffn(l, 1, T)
            self.final_out(kind, T0, T)
        for i in range(len(S.sw)):
            S._wait("sp", "sw%d" % i, S.cnt["sw%d" % i])
        for i in range(len(S.dsem)):
            S._wait("sp", "d%d" % i, S.cnt["d%d" % i])

    def attn_core(self, M, qaps, segs, out_ap, out_buf, wk):
        S = self.S
        S_all, b_sa, Pn, b_pn, PT, b_pt, c0, bs_ = wk
        off = 0
        for (g, KT, kb, tb, tbb, vbl) in segs:
            ncols = KT.shape[1]
            qa, qb = qaps[g]
            for j0 in range(0, ncols, 512):
                n = min(512, ncols - j0)
                ps, pb = self.getps()
                S.op("pe", lambda e, ps=ps, qa=qa, KT=KT, j0=j0, n=n: e.matmul(ps[:M, :n], qa, KT[:, j0:j0 + n], start=True, stop=True),
                     reads=[qb, kb], writes=[pb])
                S.op("dve", lambda e, ps=ps, tb=tb, j0=j0, n=n, off=off: e.scalar_tensor_tensor(
                    S_all[:M, off + j0:off + j0 + n], ps[:M, :n], ATT_SCALE, tb[:, j0:j0 + n], ALU.mult, ALU.add),
                    reads=[pb, tbb], writes=[b_sa])
            off += ncols
        W = off
        sm, bs = self.sml, bs_
        S.op("dve", lambda e: e.reduce_max(sm[:M, c0:c0 + 1], S_all[:M, :W], AX.X), reads=[b_sa], writes=[bs])
        S.op("dve", lambda e: e.tensor_scalar(sm[:M, c0 + 1:c0 + 2], sm[:M, c0:c0 + 1], -1.0, None, ALU.mult), reads=[bs], writes=[bs])
        S.op("dve", lambda e: e.memset(sm[:M, c0 + 2:c0 + 3], 0.0), writes=[bs])
        S.op("act", lambda e: e.activation(S_all[:M, :W], S_all[:M, :W], AF.Exp, bias=sm[:M, c0 + 1:c0 + 2], scale=1.0, accum_out=sm[:M, c0 + 2:c0 + 3]),
             reads=[b_sa, bs], writes=[b_sa, bs])
        S.op("dve", lambda e: e.reciprocal(sm[:M, c0 + 3:c0 + 4], sm[:M, c0 + 2:c0 + 3]), reads=[bs], writes=[bs])
        S.op("dve", lambda e: e.tensor_scalar(Pn[:M, :W], S_all[:M, :W], sm[:M, c0 + 3:c0 + 4], None, ALU.mult), reads=[b_sa, bs], writes=[b_pn])
        blocks = []
        off = 0
        for (g, KT, kb, tb, tbb, vbl) in segs:
            c = off
            for (va, nk, vb) in vbl:
                blocks.append((c, nk, va, vb))
                c += nk
            off += KT.shape[1]
            assert c == off
        per = max(1, min(1024 // M, 32))
        for b0 in range(0, len(blocks), per):
            pt, ptb = self.getpst()
            nb = min(per, len(blocks) - b0)
            for j in range(nb):
                c, nk, va, vb = blocks[b0 + j]
                S.op("pe", lambda e, pt=pt, j=j, c=c, nk=nk: e.transpose(pt[:nk, j * M:(j + 1) * M], Pn[:M, c:c + nk], self.identb[:M, :M]),
                     reads=[b_pn, self.b_const], writes=[ptb])
            S.op("act", lambda e, pt=pt, b0=b0, nb=nb: e.copy(PT[:, b0 * M:(b0 + nb) * M], pt[:, :nb * M]), reads=[ptb], writes=[b_pt])
        ps, pb = self.getps()
        for bi, (c, nk, va, vb) in enumerate(blocks):
            S.op("pe", lambda e, bi=bi, nk=nk, va=va, ps=ps: e.matmul(ps[:, :M], va, PT[:nk, bi * M:(bi + 1) * M],
                                                                   start=(bi == 0), stop=(bi == len(blocks) - 1)),
                 reads=[vb, b_pt], writes=[pb])
        S.op("act", lambda e, ps=ps: e.copy(out_ap, ps[:, :M]), reads=[pb], writes=[out_buf])

    def mixers(self, l, p, kind, T0, T):
        S = self.S
        isp = kind == "p"
        xs = [(self.xT[:, c, :T], self.b_x[c]) for c in range(16)]
        hs = [(self.hT[:, c, :T], self.b_h[c]) for c in range(16)]
        self.rmsnorm_fm(xs, T, lambda c: self.gv[:, 2 + l, c:c + 1], hs, D)
        hsrc = [(self.hT[:, c, :], self.b_h[c]) for c in range(16)]
        Win = self.w_in[l]
        chunks = [(cc * 128, 128) for cc in range(4)] if isp else [(s * 4, 4) for s in range(4)]
        sm, bs = self.sml, self.b_sml
        bc = self.b_const
        brT, b_br = self.brT, self.b_br
        last_tile = isp and (T0 == SEQ - TT)

        SK = getattr(Prog, 'SKIP', set())
        if 'B' not in SK:
            self.u_phase()
            uT, b_u = self.U([4, TT], BF16, "uT")

            def fm_u(ci, ps, pb):
                S.op("act", lambda e: e.activation(uT[:, ci, :T], ps, AF.Gelu_apprx_tanh), reads=[pb], writes=[b_u])
            self.gemm(Win, C_BU, 512, 16, hsrc, T, fm=fm_u)
            vgroups = chunks if isp else [(0, TS)]
            vn, b_vn = self.U([len(vgroups), 512], BF16, "vn")
            gvt, b_gvt = self.U([512], F32, "gvt")
            junk, b_junk = self.U([512], F32, "junk")
            vnf, b_vnf = self.U([512], F32, "vnf")

            def tm_v(gi, ps, pb):
                M = vgroups[gi][1]
                S.op("act", lambda e: e.activation(gvt[:M, :], ps, AF.Gelu_apprx_tanh), reads=[pb], writes=[b_gvt])
                S.op("dve", lambda e: e.memset(sm[:M, 0:1], 0.0), writes=[bs])
                S.op("act", lambda e: e.activation(junk[:M, :], gvt[:M, :], AF.Square, accum_out=sm[:M, 0:1]), reads=[b_gvt, bs], writes=[b_junk, bs])
                S.op("act", lambda e: e.activation(sm[:M, 1:2], sm[:M, 0:1], AF.Sqrt, bias=self.epsb[:M, 0:1], scale=1.0 / 512), reads=[bs, bc], writes=[bs])
                S.op("dve", lambda e: e.reciprocal(sm[:M, 1:2], sm[:M, 1:2]), reads=[bs], writes=[bs])
                S.op("dve", lambda e: e.scalar_tensor_tensor(vnf[:M, :], gvt[:M, :], sm[:M, 1:2], self.ggm_b[:M, l, :], ALU.mult, ALU.mult),
                     reads=[b_gvt, bs, bc], writes=[b_vnf])
                S.op("act", lambda e: e.copy(vn[:M, gi, :], vnf[:M, :]), reads=[b_vnf], writes=[b_vn])
                if not isp:
                    S.dma("sp", self.gv_s[l].rearrange("s t c -> (s t) c"), vnf[:TS, :], reads=[b_vnf])
            self.gemm(Win, C_BV, 512, 16, hsrc, T, tm=tm_v, tgroups=vgroups)
            wst, b_wst = self.U([4, 128], BF16, "wst")
            if isp:
                S.dma("pool", wst[:, :, :], self.wsT[l], writes=[b_wst])
            else:
                S.dma("pool", wst[:TS, :, :TS], self.wsT_s[l], writes=[b_wst])
            t1, b_t1 = self.U([128], F32, "t1")
            for gi, (t0, M) in enumerate(vgroups):
                for g in range(4):
                    ps, pb = self.getps()
                    S.op("pe", lambda e, ps=ps, gi=gi, g=g, M=M: e.matmul(ps[:, :M], vn[:M, gi, g * 128:(g + 1) * 128], wst[:M, g, :M], start=True, stop=True),
                         reads=[b_vn, b_wst], writes=[pb])
                    bias = self.bsp_b[:, l, g * 128:g * 128 + M] if isp else self.bsps_b[:, l, g * TS:(g + 1) * TS]
                    S.op("dve", lambda e, ps=ps, bias=bias, M=M: e.tensor_tensor(t1[:, :M], ps[:, :M], bias, ALU.add), reads=[pb, bc], writes=[b_t1])
                    S.op("dve", lambda e, g=g, t0=t0, M=M: e.tensor_tensor(brT[:, 4 + g, t0:t0 + M], t1[:, :M], uT[:, g, t0:t0 + M], ALU.mult),
                         reads=[b_t1, b_u], writes=[b_br[4 + g]])

        if 'C' not in SK:
            self.u_phase()
            NSQ = 1 if isp else 4
            Tn = T if isp else 4
            L = 15 + Tn
            cin, b_cin = self.U([4, NSQ * L], F32, "cin")
            cin4 = cin.rearrange("p g (s j) -> p g s j", j=L)
            if isp:
                if T0 == 0:
                    S.op("dve", lambda e: e.memset(cin4[:, :, 0, 0:15], 0.0), writes=[b_cin])
                else:
                    S.op("dve", lambda e: e.tensor_copy(cin4[:, :, 0, 0:15], self.halo[:, l, :, :]), reads=[self.b_halo[l]], writes=[b_cin])
            else:
                spin, b_spin = self.U([NSEQ_S, 512], F32, "spin")
                S.dma("sp", spin[:15, :, :], self.spool[l].rearrange("s r c -> r s c"), writes=[b_spin])
                for g in range(4):
                    ps, pb = self.getps()
                    for s in range(4):
                        S.op("pe", lambda e, ps=ps, s=s, g=g: e.transpose(ps[:, s * 16:s * 16 + 15], spin[:15, s, g * 128:(g + 1) * 128], self.ident[:15, :15]),
                             reads=[b_spin, bc], writes=[pb])
                    S.op("act", lambda e, ps=ps, g=g: e.copy(cin4[:, g, :, 0:15], ps[:, 0:64].rearrange("p (s j) -> p s j", j=16)[:, :, 0:15]),
                         reads=[pb], writes=[b_cin])
                S.dma("sp", self.pool_s[l][:, 0:11, :], self.spool[l][:, 4:15, :])

            def fm_c(ci, ps, pb):
                if isp:
                    S.op("act", lambda e: e.copy(cin4[:, ci, 0, 15:15 + T], ps), reads=[pb], writes=[b_cin])
                else:
                    S.op("act", lambda e: e.copy(cin4[:, ci, :, 15:19], ps.rearrange("p (s t) -> p s t", t=4)), reads=[pb], writes=[b_cin])
            ctm, b_ctm = self.U([512], F32, "ctm")
            ctg = [(384, 128)] if isp else chunks

            def tm_c(gi, ps, pb):
                M = ctg[gi][1]
                S.op("act", lambda e: e.copy(ctm[:M, :], ps), reads=[pb], writes=[b_ctm])
                if isp:
                    S.dma("sp", self.pool_p[l], ctm[113:128, :], reads=[b_ctm])
                else:
                    S.dma("sp", self.pool_s[l][gi, 11:15, :], ctm[:4, :], reads=[b_ctm])
            self.gemm(Win, C_C, 512, 16, hsrc, T, fm=fm_c, tm=(tm_c if (last_tile or not isp) else None), tgroups=ctg)
            wa, b_wa = self.U([NSQ * L], F32, "wa")
            wb_, b_wb = self.U([NSQ * L], F32, "wb")
            S.op("dve", lambda e: e.memset(wa[:, :], 0.0), writes=[b_wa])
            S.op("dve", lambda e: e.memset(wb_[:, :], 0.0), writes=[b_wb])
            wa3 = wa.rearrange("p (s j) -> p s j", j=L)
            wb3 = wb_.rearrange("p (s j) -> p s j", j=L)
            dif, b_dif = self.U([TT], BF16, "dif")
            tmpf, b_tmpf = self.U([TT], F32, "tmpf")
            ict, b_ict = self.U([4, TT], F32, "ict")
            S.dma("sp", ict[:, :, :], self.pool_ic[p], writes=[b_ict])
            wpl, b_wpl = self.U([4, 128], BF16, "wpl")
            S.dma("pool", wpl[:, :, :], self.wpool[l], writes=[b_wpl])
            for g in range(4):
                cur, curb = cin4[:, g], b_cin
                pp = [(wa3, b_wa), (wb3, b_wb)]
                for k in range(g + 1):
                    sh = 1 << k
                    dst, dstb = pp[k % 2]
                    S.op("dve", lambda e, dst=dst, cur=cur, sh=sh: e.tensor_tensor(dst[:, :, sh:L], cur[:, :, sh:L], cur[:, :, 0:L - sh], ALU.add),
                         reads=[curb], writes=[dstb])
                    cur, curb = dst, dstb
                if isp:
                    tcur = cur[:, 0, 15:L]; xcur = cin4[:, g, 0, 15:L]; icv = ict[:, g, :T]; tf = tmpf[:, :T]; df = dif[:, :T]
                else:
                    tcur = cur[:, :, 15:L]; xcur = cin4[:, g, :, 15:L]
                    icv = ict[:, g, 0:TS].rearrange("p (s t) -> p s t", t=4)
                    tf = tmpf[:, :TS].rearrange("p (s t) -> p s t", t=4); df = dif[:, :TS].rearrange("p (s t) -> p s t", t=4)
                S.op("dve", lambda e, tf=tf, tcur=tcur, icv=icv: e.tensor_tensor(tf, tcur, icv, ALU.mult), reads=[curb, b_ict], writes=[b_tmpf])
                S.op("dve", lambda e, tf=tf, df=df, xcur=xcur: e.tensor_tensor(df, tf, xcur, ALU.subtract), reads=[b_tmpf, b_cin], writes=[b_dif])
                ps, pb = self.getps()
                S.op("pe", lambda e, ps=ps, g=g: e.matmul(ps[:, :T], wpl[:, g, :], dif[:, :T], start=True, stop=True), reads=[b_wpl, b_dif], writes=[pb])
                S.op("act", lambda e, ps=ps, g=g: e.mul(brT[:, 8 + g, :T], ps[:, :T], self.gs[:, l, 0, g:g + 1]), reads=[pb, bc], writes=[b_br[8 + g]])
            if isp:
                S.op("dve", lambda e: e.tensor_copy(self.halo[:, l, :, :], cin4[:, :, 0, L - 15:L]), reads=[b_cin], writes=[self.b_halo[l]])

        if 'D' not in SK:
            self.u_phase()
            cc_t, b_cc = self.U([TT], F32, "cc")
            ss_t, b_ss = self.U([TT], F32, "ss")
            S.dma("sp", cc_t[:, :], self.rot_cc[p], writes=[b_cc])
            S.dma("sp", ss_t[:, :], self.rot_ss[p], writes=[b_ss])
            qa, b_qa = self.U([4, TT], F32, "qa")
            rq, b_rq = self.U([4, TT], BF16, "rq")
            rk, b_rk = self.U([4, TT], BF16, "rk")
            r1, b_r1 = self.U([TT], F32, "r1")
            r2, b_r2 = self.U([TT], F32, "r2")
            for (cA, cS, dst, b_dst) in ((C_DQ, C_DQS, rq, b_rq), (C_DK, C_DKS, rk, b_rk)):
                def fm_a(ci, ps, pb):
                    S.op("act", lambda e: e.copy(qa[:, ci, :T], ps), reads=[pb], writes=[b_qa])
                self.gemm(Win, cA, 512, 16, hsrc, T, fm=fm_a)

                def fm_s(ci, ps, pb, dst=dst, b_dst=b_dst):
                    S.op("dve", lambda e: e.tensor_tensor(r1[:, :T], qa[:, ci, :T], cc_t[:, :T], ALU.mult), reads=[b_qa, b_cc], writes=[b_r1])
                    S.op("dve", lambda e: e.tensor_tensor(r2[:, :T], ps, ss_t[:, :T], ALU.mult), reads=[pb, b_ss], writes=[b_r2])
                    S.op("dve", lambda e: e.tensor_tensor(dst[:, ci, :T], r1[:, :T], r2[:, :T], ALU.add), reads=[b_r1, b_r2], writes=[b_dst])
                self.gemm(Win, cS, 512, 16, hsrc, T, fm=fm_s)
            vt, b_vt = self.U([len(chunks), 512], BF16, "vt")

            def tm_dv(gi, ps, pb):
                M = chunks[gi][1]
                S.op("act", lambda e: e.mul(vt[:M, gi, :], ps, ATT_SCALE), reads=[pb], writes=[b_vt])
            self.gemm(Win, C_DV, 512, 16, hsrc, T, tm=tm_dv, tgroups=chunks)
            sg, b_sg = self.U([4, TT], BF16, "sg")

            def fm_g(ci, ps, pb):
                S.op("act", lambda e: e.activation(sg[:, ci, :T], ps, AF.Silu), reads=[pb], writes=[b_sg])
            self.gemm(Win, C_DG, 512, 16, hsrc, T, fm=fm_g)
            oT, b_oT = self.U([4, TT], F32, "oT")
            rtmp = [[self.U([128], BF16, "rt%d%d" % (h_, j_)) for j_ in range(4)] for h_ in range(4)]
            if isp:
                dt_, qd_, kd_, cd = self.dtp, self.qdp, self.kdp, self.cst["ret_cd_p"]
            else:
                dt_, qd_, kd_, cd = self.dts, self.qds, self.kds, self.cst["ret_cd_s"]
                Ssms = [self.U([128], F32, "Ssm%d" % h_) for h_ in range(4)]
                Ssbs = [self.U([128], BF16, "Ssb%d" % h_) for h_ in range(4)]
            for gi, (t0, M) in enumerate(chunks):
                for h in range(4):
                    (atd, b_atd), (rqd, b_rqd), (rkt, b_rkt), (vk, b_vk) = rtmp[h]
                    if isp:
                        Sf, Sb_, bS, bSb = self.Sst[:, l, h, :], self.Sbf[:, l, h, :], self.b_S[l][h], self.b_Sb[l][h]
                        if T0 == 0 and gi == 0:
                            S.op("dve", lambda e, Sf=Sf: e.memset(Sf, 0.0), writes=[bS])
                            S.op("dve", lambda e, Sb_=Sb_: e.memset(Sb_, 0.0), writes=[bSb])
                    else:
                        Sf, Sb_, bS, bSb = Ssms[h][0][:, :], Ssbs[h][0][:, :], Ssms[h][1], Ssbs[h][1]
                        S.dma("sp", Sf, self.sret[l, gi, h], writes=[bS])
                        S.op("act", lambda e, Sf=Sf, Sb_=Sb_: e.copy(Sb_, Sf), reads=[bS], writes=[bSb])
                    ps, pb = self.getps()
                    S.op("pe", lambda e, ps=ps, h=h, t0=t0, M=M: e.matmul(ps[:M, :M], rk[:, h, t0:t0 + M], rq[:, h, t0:t0 + M], start=True, stop=True),
                         reads=[b_rk, b_rq], writes=[pb])
                    S.op("dve", lambda e, ps=ps, h=h, M=M: e.tensor_tensor(atd[:M, :M], ps[:M, :M], dt_[:M, h, :M], ALU.mult), reads=[pb, bc], writes=[b_atd])
                    S.op("dve", lambda e, h=h, t0=t0, M=M: e.tensor_tensor(rqd[:, :M], rq[:, h, t0:t0 + M], qd_[:, h, :M], ALU.mult), reads=[b_rq, bc], writes=[b_rqd])
                    ps2, pb2 = self.getps()
                    S.op("pe", lambda e, ps2=ps2, gi=gi, h=h, M=M: e.matmul(ps2[:, :M], vt[:M, gi, h * 128:(h + 1) * 128], atd[:M, :M], start=True, stop=False),
                         reads=[b_vt, b_atd], writes=[pb2])
                    S.op("pe", lambda e, ps2=ps2, Sb_=Sb_, M=M: e.matmul(ps2[:, :M], Sb_, rqd[:, :M], start=False, stop=True), reads=[bSb, b_rqd], writes=[pb2])
                    S.op("act", lambda e, ps2=ps2, h=h, t0=t0, M=M: e.copy(oT[:, h, t0:t0 + M], ps2[:, :M]), reads=[pb2], writes=[b_oT])
                    pt, ptb = self.getpst()
                    S.op("pe", lambda e, pt=pt, h=h, t0=t0, M=M: e.transpose(pt[:M, :128], rk[:, h, t0:t0 + M], self.identb[:, :]), reads=[b_rk, bc], writes=[ptb])
                    S.op("act", lambda e, pt=pt, M=M: e.copy(rkt[:M, :], pt[:M, :128]), reads=[ptb], writes=[b_rkt])
                    S.op("dve", lambda e, gi=gi, h=h, M=M: e.tensor_scalar(vk[:M, :], vt[:M, gi, h * 128:(h + 1) * 128], kd_[:M, h:h + 1], None, ALU.mult),
                         reads=[b_vt, bc], writes=[b_vk])
                    ps3, pb3 = self.getps()
                    S.op("pe", lambda e, ps3=ps3, M=M: e.matmul(ps3[:, :128], rkt[:M, :], vk[:M, :], start=True, stop=True), reads=[b_rkt, b_vk], writes=[pb3])
                    S.op("dve", lambda e, ps3=ps3, Sf=Sf, h=h: e.scalar_tensor_tensor(Sf, Sf, cd[h], ps3[:, :128], ALU.mult, ALU.add), reads=[bS, pb3], writes=[bS])
                    if isp:
                        S.op("act", lambda e, Sf=Sf, Sb_=Sb_: e.copy(Sb_, Sf), reads=[bS], writes=[bSb])
                        if last_tile and gi == 3:
                            S.dma("sp", self.ret_p[l, h], Sf, reads=[bS])
                    else:
                        S.dma("sp", self.ret_s[l, gi, h], Sf, reads=[bS])
            for h in range(4):
                sq, sqb = self.sq[h % 2], self.b_sq[h % 2]
                S.op("act", lambda e, sq=sq, h=h: e.activation(sq[:, :T], oT[:, h, :T], AF.Square), reads=[b_oT], writes=[sqb])
                ps, pb = self.getps()
                S.op("pe", lambda e, ps=ps, sq=sq: e.matmul(ps[:, :T], self.onesb[:, :], sq[:, :T], start=True, stop=True), reads=[sqb, bc], writes=[pb])
                S.op("act", lambda e, ps=ps: e.activation(self.rstd[:, :T], ps[:, :T], AF.Sqrt, bias=self.epsb[:, 0:1], scale=1.0 / 128),
                     reads=[pb, bc], writes=[self.b_rstd])
                S.op("dve", lambda e: e.reciprocal(self.rstd[:, :T], self.rstd[:, :T]), reads=[self.b_rstd], writes=[self.b_rstd])
                S.op("dve", lambda e, h=h: e.scalar_tensor_tensor(r1[:, :T], oT[:, h, :T], self.gs[:, l, 1, h:h + 1], self.rstd[:, :T], ALU.mult, ALU.mult),
                     reads=[b_oT, self.b_rstd, bc], writes=[b_r1])
                S.op("dve", lambda e, h=h: e.tensor_tensor(brT[:, 12 + h, :T], r1[:, :T], sg[:, h, :T], ALU.mult), reads=[b_r1, b_sg], writes=[b_br[12 + h]])

        if 'A' not in SK:
            self.u_phase()
            QT, b_q = self.QT, self.b_q

            def fm_q(ci, ps, pb):
                S.op("act", lambda e: e.copy(QT[:, ci, :T], ps), reads=[pb], writes=[b_q[ci]])
            AQ = getattr(Prog, "AQ", 15)
            for g in range(3):
                if AQ & 1:
                    self.gemm(Win, C_AQ + g * 512, 512, 16, hsrc, T, fm=fm_q, ci0=g * 4)
            kfs = [self.U([512], F32, "kf%d" % i) for i in range(2)]
            kfn = [0]
            if isp:
                kts = [self.U([TT], BF16, "kt%d" % i) for i in range(2)]
                vbs = [self.U([512], BF16, "vb%d" % i) for i in range(2)]
            else:
                KTn, b_KTn = self.U([12, TS], BF16, "KTn")
                Vn, b_Vn = self.U([NSEQ_S, 1536], BF16, "Vn")
            for g in range(3):
                keep = min(WINS[g], SEQ)

                def fm_k(ci, ps, pb, g=g):
                    h = ci - 4 * g
                    if isp:
                        kt, ktb = kts[ci % 2]
                        S.op("act", lambda e: e.copy(kt[:, :T], ps), reads=[pb], writes=[ktb])
                        S.dma("sp", self.kT_hist[l, g, h][:, T0:T0 + T], kt[:, :T], reads=[ktb], writes=[self.b_kth[l][g]])
                    else:
                        S.op("act", lambda e: e.copy(KTn[:, ci, :T], ps), reads=[pb], writes=[b_KTn if not getattr(Prog, "HYP", 0) else Buf("tmpk")])
                if isp:
                    ktg = [(t0, M) for (t0, M) in chunks if T0 + t0 >= SEQ - keep]
                else:
                    ktg = chunks

                def tm_kv(gi, ps, pb, g=g, which=0, grp=None):
                    t0, M = grp[gi]
                    kf, kfb = kfs[kfn[0] % 2]
                    kfn[0] += 1
                    S.op("act", lambda e: e.copy(kf[:M, :], ps), reads=[pb], writes=[kfb])
                    if isp:
                        r0 = T0 + t0 - (SEQ - keep)
                        if r0 >= 0:
                            S.dma("sp", self.kv_p[g][l, r0:r0 + M, which].rearrange("r h d -> r (h d)"), kf[:M, :], reads=[kfb])
                    else:
                        S.dma("sp", self.kv_s[g][l, gi, :, which].rearrange("t h d -> t (h d)"), kf[:M, :], reads=[kfb])
                if AQ & 6:
                    self.gemm(Win, C_AK + g * 512, 512, 16, hsrc, T, fm=(fm_k if AQ & 2 else None),
                              tm=((lambda gi, ps, pb, g=g, ktg=ktg: tm_kv(gi, ps, pb, g=g, which=0, grp=ktg)) if (ktg and (AQ & 4)) else None),
                              tgroups=ktg, ci0=g * 4)

                def tm_v(gi, ps, pb, g=g):
                    t0, M = chunks[gi]
                    if (not isp) or (T0 + t0 >= SEQ - keep):
                        tm_kv(gi, ps, pb, g=g, which=1, grp=chunks)
                    if isp:
                        vb, vbb = vbs[gi % 2]
                        S.op("dve", lambda e: e.tensor_copy(vb[:M, :], ps), reads=[pb], writes=[vbb])
                        S.dma("sp", self.v_hist[l, g][T0 + t0:T0 + t0 + M, :], vb[:M, :], reads=[vbb], writes=[self.b_vh[l][g]])
                    else:
                        if getattr(Prog, "HYP", 0) != 2:
                            S.op("dve", lambda e: e.tensor_copy(Vn[:M, gi, g * 512:(g + 1) * 512], ps), reads=[pb], writes=[b_Vn])
                if AQ & 8:
                    self.gemm(Win, C_AV + g * 512, 512, 16, hsrc, T, tm=tm_v, tgroups=chunks)

            AST = getattr(Prog, "AST", 3)
            allw = [b_ for i_ in range(NWSLOT) for b_ in self.b_w[i_]]
            S.op("dve", lambda e: e.memset(self.sml[:1, 15:16], 0.0), writes=allw)
            if AST < 2:
                pass
            elif isp:
                self.u_phase()
                KTw, b_KTw = self.U([3712], BF16, "KTw")
                Vw, b_Vw = self.U([29, 128], BF16, "Vw")
                tb, b_tb = self.U([TBL_W], BF16, "tb")
                wkA = self.U([TBL_W], F32, "S_all") + self.U([TBL_W], BF16, "Pn") + self.U([TBL_W], BF16, "PT") + (2, self.b_sml)
                wkB = (self.wslot[0][:, :2 * TBL_W].bitcast(F32), self.b_w[0][0], self.wslot[1][:, :TBL_W], self.b_w[1][0],
                       self.wslot[1][:, TBL_W:2 * TBL_W], self.b_w[1][1], 8, self.b_sml2)
                wks = [wkA, wkB]
                unit = [0]
                KVs = [(KTw, b_KTw, Vw, b_Vw),
                       (self.wslot[2][:, :3712], self.b_w[2][0], self.wslot[2][:, 3712:3712 + 29 * 128].rearrange("p (c d) -> p c d", d=128), self.b_w[2][1])]
                koff = (0, 640, 1664)
                voff = (0, 5, 13)
                for h in range(4):
                    KTw, b_KTw, Vw, b_Vw = KVs[h % 2]
                    S.dma("pool", tb[:, :], self.tbl[h], writes=[b_tb])
                    los = []
                    for g in range(3):
                        lo = max(0, T0 - WINS[g])
                        los.append(lo)
                        n = T0 + TT - lo
                        S.dma("sp", KTw[:, koff[g]:koff[g] + n], self.kT_hist[l, g, h][:, lo:T0 + TT], reads=[self.b_kth[l][g]], writes=[b_KTw])
                        S.dma("sp", Vw[:, voff[g]:voff[g] + n // 128, :],
                              self.v_hist[l, g][lo:T0 + TT, h * 128:(h + 1) * 128].rearrange("(c p) d -> p c d", p=128),
                              reads=[self.b_vh[l][g]], writes=[b_Vw])
                    for (t0, M) in chunks:
                        P0 = T0 + t0
                        segs = []
                        for g in range(3):
                            win = WINS[g]
                            lo_c = max(0, P0 - win)
                            ncols = P0 + 128 - lo_c
                            ks = koff[g] + (lo_c - los[g])
                            tcol = TBL_OFF[g] + (win + 128 - ncols)
                            vbl = [(Vw[:, voff[g] + (lo_c - los[g]) // 128 + j, :], 128, b_Vw) for j in range(ncols // 128)]
                            segs.append((g, KTw[:, ks:ks + ncols], b_KTw, tb[:, tcol:tcol + ncols], b_tb, vbl))
                        qaps = [(QT[:, g * 4 + h, t0:t0 + M], b_q[g * 4 + h]) for g in range(3)]
                        if AST >= 3:
                            self.attn_core(M, qaps, segs, brT[:, h, t0:t0 + M], b_br[h], wks[unit[0] % 2])
                            unit[0] += 1
            else:
                Kc, b_Kc = self.U([16, 128], BF16, "Kc")
                Vw, b_Vw = self.U([21, 128], BF16, "Vw")
                KTw, b_KTw = self.U([2704], BF16, "KTw")
                tb, b_tb = self.U([TBL_W], BF16, "tb")
                wkA = self.U([2704], F32, "S_all") + self.U([2704], BF16, "Pn") + self.U([128], BF16, "PT") + (2, self.b_sml)
                wkB = (self.wslot[0][:, :2 * 2704].bitcast(F32), self.b_w[0][0], self.wslot[1][:, :2704], self.b_w[1][0],
                       self.wslot[1][:, 2704:2704 + 128], self.b_w[1][1], 8, self.b_sml2)
                wks = [wkA, wkB]
                unit = [0]
                koff = (0, 132, 648)
                voff = (0, 1, 5)
                for h in range(4):
                    S.dma("pool", tb[:, :], self.tbl[h], writes=[b_tb])
                    for (t0, M) in chunks:
                        s = t0 // 4
                        segs = []
                        for g in range(3):
                            win = WINS[g]
                            nch = win // 128
                            S.dma("pool", Kc[:, :nch, :], self.ckv[g][l, s, :, 0, h, :].rearrange("(c p) d -> p c d", p=128), writes=[b_Kc])
                            S.dma("pool", Vw[:, voff[g]:voff[g] + nch, :], self.ckv[g][l, s, :, 1, h, :].rearrange("(c p) d -> p c d", p=128), writes=[b_Vw])
                            for j0 in range(0, nch, 8):
                                pt, ptb = self.getpst()
                                nb = min(8, nch - j0)
                                for j in range(nb):
                                    S.op("pe", lambda e, pt=pt, j=j, j0=j0: e.transpose(pt[:, j * 128:(j + 1) * 128], Kc[:, j0 + j, :], self.identb[:, :]),
                                         reads=[b_Kc, bc], writes=[ptb])
                                S.op("act", lambda e, pt=pt, j0=j0, nb=nb, g=g: e.copy(KTw[:, koff[g] + j0 * 128:koff[g] + (j0 + nb) * 128], pt[:, :nb * 128]),
                                     reads=[ptb], writes=[b_KTw])
                            S.op("act", lambda e, g=g, t0=t0: e.copy(KTw[:, koff[g] + win:koff[g] + win + 4], KTn[:, g * 4 + h, t0:t0 + 4]),
                                 reads=[b_KTn], writes=[b_KTw])
                            vbl = [(Vw[:, voff[g] + j, :], 128, b_Vw) for j in range(nch)]
                            vbl.append((Vn[:4, s, g * 512 + h * 128:g * 512 + (h + 1) * 128], 4, b_Vn))
                            segs.append((g, KTw[:, koff[g]:koff[g] + win + 4], b_KTw, tb[:M, TBL_OFF[g]:TBL_OFF[g] + win + 4], b_tb, vbl))
                        qaps = [(QT[:, g * 4 + h, t0:t0 + M], b_q[g * 4 + h]) for g in range(3)]
                        if AST >= 3:
                            self.attn_core(M, qaps, segs, brT[:, h, t0:t0 + M], b_br[h], wks[unit[0] % 2])
                            unit[0] += 1

        if 'A' not in SK:
            S.op("dve", lambda e: e.memset(self.sml[:1, 15:16], 0.0), writes=allw)
        if 'M' not in SK:
            self.u_phase()
            mg, _ = self.U([16, TT], BF16, "mg")
            b_mg = [Buf("mg%d" % i) for i in range(16)]
            for b in b_mg:
                b.r = dict(self.ustate)
            self.ubufs.extend(b_mg)
            acc, b_acc = self.U([TT], F32, "acc")
            sgt, b_sgt = self.U([TT], F32, "sgt")
            tt, b_tt = self.U([TT], F32, "tt")
            Wb = self.w_br[l]
            gate_v = self.w_in[l][:, C_GATE:C_GATE + 8192].rearrange("(kc p) (n d) -> p kc n d", p=128, n=4)
            for dc in range(16):
                wg_, wgb = self.wload(gate_v[:, :, :, dc * 128:(dc + 1) * 128])
                wb2, wbb = self.wload(Wb[:, dc * 128:(dc + 1) * 128].rearrange("(j p) d -> p j d", p=128))
                for n in range(4):
                    psg, pgb = self.getps()
                    for kc in range(16):
                        S.op("pe", lambda e, psg=psg, kc=kc, n=n: e.matmul(psg[:, :T], wg_[:, kc, n, :], self.hT[:, kc, :T], start=(kc == 0), stop=(kc == 15)),
                             reads=[wgb[n], self.b_h[kc]], writes=[pgb])
                    psp, ppb = self.getps()
                    for kc in range(4):
                        S.op("pe", lambda e, psp=psp, kc=kc, n=n: e.matmul(psp[:, :T], wb2[:, 4 * n + kc, :], brT[:, 4 * n + kc, :T], start=(kc == 0), stop=(kc == 3)),
                             reads=wbb + [b_br[4 * n + kc]], writes=[ppb])
                    S.op("act", lambda e, psg=psg: e.activation(sgt[:, :T], psg[:, :T], AF.Sigmoid), reads=[pgb], writes=[b_sgt])
                    if n == 0:
                        S.op("dve", lambda e, psp=psp: e.tensor_tensor(acc[:, :T], sgt[:, :T], psp[:, :T], ALU.mult), reads=[b_sgt, ppb], writes=[b_acc])
                    else:
                        S.op("dve", lambda e, psp=psp: e.tensor_tensor(tt[:, :T], sgt[:, :T], psp[:, :T], ALU.mult), reads=[b_sgt, ppb], writes=[b_tt])
                        if n < 3:
                            S.op("dve", lambda e: e.tensor_tensor(acc[:, :T], acc[:, :T], tt[:, :T], ALU.add), reads=[b_acc, b_tt], writes=[b_acc])
                        else:
                            S.op("dve", lambda e, dc=dc: e.tensor_tensor(mg[:, dc, :T], acc[:, :T], tt[:, :T], ALU.add), reads=[b_acc, b_tt], writes=[b_mg[dc]])
            mgsrc = [(mg[:, kc, :], b_mg[kc]) for kc in range(16)]

            def fo(ci, ps, pb):
                S.op("dve", lambda e: e.tensor_tensor(self.xT[:, ci, :T], self.xT[:, ci, :T], ps, ALU.add), reads=[pb, self.b_x[ci]], writes=[self.b_x[ci]])
            for j0 in range(0, D, 512):
                self.gemm(self.w_out[l], j0, 512, 16, mgsrc, T, fm=fo, ci0=j0 // 128)


_PROG = None


def _layout_inputs(inp):
    f = lambda a: np.ascontiguousarray(np.asarray(a, dtype=np.float32))
    w_in = f(inp["w_in"])
    perm = np.concatenate([h * 128 + (np.arange(128) + 64) % 128 for h in range(4)])
    w_in_ext = np.concatenate([w_in, w_in[:, :, C_DQ + perm], w_in[:, :, C_DK + perm]], axis=2)
    shared = {
        "w_ffn1_gate": f(inp["w_ffn1_gate"]), "w_ffn1_up": f(inp["w_ffn1_up"]), "w_ffn1_down": f(inp["w_ffn1_down"]),
        "w_ffn2_gate": f(inp["w_ffn2_gate"]), "w_ffn2_up": f(inp["w_ffn2_up"]), "w_ffn2_down": f(inp["w_ffn2_down"]),
        "w_in": np.ascontiguousarray(w_in_ext), "w_branch": f(inp["w_branch"]).reshape(NL, 2048, D), "w_out": f(inp["w_out"]),
    }
    vecs = np.stack([f(inp["g_ffn1"])[0], f(inp["g_ffn1"])[1], f(inp["g_mix"])[0], f(inp["g_mix"])[1],
                     f(inp["g_ffn2"])[0], f(inp["g_ffn2"])[1], f(inp["g_final"])])
    shared["gvec"] = np.ascontiguousarray(vecs.reshape(7, 16, 128).transpose(2, 0, 1))
    gsm = np.stack([f(inp["pool_scale"]).reshape(NL, 4, 128), f(inp["g_ret"]).reshape(NL, 4, 128)], axis=1)
    shared["gsm"] = np.ascontiguousarray(gsm.transpose(3, 0, 1, 2))
    shared["ggm"] = f(inp["g_gmlp"])
    bsp = f(inp["b_spatial"])
    shared["bsp"] = np.ascontiguousarray(bsp.reshape(NL, 512))
    shared["bsp_s"] = np.ascontiguousarray(np.tile(bsp[:, :, None, :4], (1, 1, 4, 1)).reshape(NL, 4 * TS))
    wsp = f(inp["w_spatial"])
    tril = np.tril(np.ones((128, 128), bool))
    wtr = np.where(tril[None, None], wsp, np.float32(0))
    shared["wsT"] = np.ascontiguousarray(wtr.transpose(0, 3, 1, 2))
    wss = np.zeros((NL, TS, 4, TS), np.float32)
    for q in range(4):
        wss[:, q * 4:(q + 1) * 4, :, q * 4:(q + 1) * 4] = wtr[:, :, :4, :4].transpose(0, 3, 1, 2)
    shared["wsT_s"] = wss
    shared["wpool"] = np.ascontiguousarray(f(inp["w_pool"]).transpose(0, 2, 1, 3))
    shared["tbl"] = host_tables(f(inp["rel_bias"]))
    cst = host_consts()
    for k, v in cst.items():
        if isinstance(v, np.ndarray):
            shared[k] = v
    xp = f(inp["x_prompt"]); xs = f(inp["x_sample"])
    caches = [f(inp["cache_attn_kv_w128"]), f(inp["cache_attn_kv_w512"]), f(inp["cache_attn_kv_w2048"])]
    spool = f(inp["state_pool"]); sret = f(inp["state_ret"])
    maps = []
    for c in range(8):
        m = dict(shared)
        m["xp"] = xp[c % 4]
        m["xs"] = np.ascontiguousarray(xs[4 * c:4 * c + 4].reshape(TS, D))
        for g in range(3):
            m["ckv%d" % g] = np.ascontiguousarray(caches[g][:, 4 * c:4 * c + 4])
        m["spool"] = np.ascontiguousarray(spool[:, 4 * c:4 * c + 4])
        m["sret"] = np.ascontiguousarray(sret[:, 4 * c:4 * c + 4])
        maps.append(m)
    return maps


def kernel(**inputs):
    global _PROG
    if _PROG is None:
        _PROG = Prog()
    pr = _PROG
    maps = _layout_inputs(inputs)
    for m in maps:
        assert set(m.keys()) == set(pr.input_specs.keys()), (set(m.keys()) ^ set(pr.input_specs.keys()))
    res = run_bass_kernel_spmd(pr.nc, maps, core_ids=list(range(8))).results
    B = 4
    y_p = np.stack([res[b]["y_p"] for b in range(B)])
    y_s = np.concatenate([res[c]["y_s"].reshape(4, 4, D) for c in range(8)], axis=0)
    kvp = [np.stack([res[b]["kv_p%d" % g] for b in range(B)], axis=1) for g in range(3)]
    kvs = [np.concatenate([res[c]["kv_s%d" % g] for c in range(8)], axis=1) for g in range(3)]
    pool_p = np.stack([res[b]["pool_p"] for b in range(B)], axis=1)
    pool_s = np.concatenate([res[c]["pool_s"] for c in range(8)], axis=1)
    ret_p = np.stack([res[b]["ret_p"] for b in range(B)], axis=1)
    ret_s = np.concatenate([res[c]["ret_s"] for c in range(8)], axis=1)
    gv_s = np.concatenate([res[c]["gv_s"] for c in range(8)], axis=1)
    outs = (y_p, y_s, kvp[0], kvp[1], kvp[2], kvs[0], kvs[1], kvs[2], pool_p, pool_s, ret_p, ret_s, gv_s)
    return tuple(np.ascontiguousarray(o, dtype=np.float32) for o in outs)
```

```python
import numpy as np
import concourse.bass as bass
import concourse.mybir as mybir
from concourse.bass_utils import run_bass_kernel_spmd

F32 = mybir.dt.float32
BF16 = mybir.dt.bfloat16
ALU = mybir.AluOpType
AF = mybir.ActivationFunctionType
AX = mybir.AxisListType


class Buf:
    __slots__ = ("name", "w", "r", "ps")

    def __init__(self, name, ps=False):
        self.name = name
        self.ps = ps
        self.w = None
        self.r = {}


class Sched:
    SIM = False

    def __init__(self, nc, n_dma_sems=40):
        self.nc = nc
        self.eng = {"pe": nc.tensor, "act": nc.scalar, "dve": nc.vector,
                    "pool": nc.gpsimd, "sp": nc.sync}
        self.sems = {}
        self.cnt = {}
        for e in self.eng:
            self.sems[e] = nc.alloc_semaphore("s_" + e)
            self.cnt[e] = 0
        self.dsem = [nc.alloc_semaphore("d%d" % i) for i in range(n_dma_sems)]
        for i in range(n_dma_sems):
            self.sems["d%d" % i] = self.dsem[i]
            self.cnt["d%d" % i] = 0
        self.dnext = 0
        self.seen = {e: {} for e in self.eng}
        self.ninst = 0
        self.last_drain = {}
        self.sw = [nc.alloc_semaphore("sw%d" % i) for i in range(24)]
        for i in range(24):
            self.sems["sw%d" % i] = self.sw[i]
            self.cnt["sw%d" % i] = 0
        self.swn = 0
        self.swfifo = []
        self.pool_issued = 0

    def _relay(self):
        j = self.swfifo.pop(0)
        p = self.eng["pool"]
        p.wait_ge(self.sw[j], 16)
        p.sem_inc(self.sw[j], -16)
        p.sem_inc(self.sems["pool"], 1)
        self.cnt["pool"] += 1
        self.ninst += 3

    def flush_pool(self):
        while self.swfifo:
            self._relay()

    def dma_pool(self, out, in_, reads=(), writes=(), wclass=False):
        j = self.swn
        self.swn = (j + 1) % len(self.sw)
        key = "sw%d" % j
        self._wait("pool", key, self.cnt[key])
        self._deps("pool", reads, writes)
        ins = self.eng["pool"].dma_start(out=out, in_=in_)
        self.cnt[key] += 16
        ins.then_inc(self.sw[j], 16)
        self._mark((key, self.cnt[key]), reads, writes)
        self.ninst += 1
        return ins

    def _wait(self, e, key, val):
        if val <= 0:
            return
        if self.seen[e].get(key, 0) >= val:
            return
        self.seen[e][key] = val
        self.eng[e].wait_ge(self.sems[key], val)
        self.ninst += 1

    def _deps(self, e, reads, writes):
        need = {}

        def add(t):
            if t is None:
                return
            k, v = t
            if need.get(k, 0) < v:
                need[k] = v
        for b in reads:
            add(b.w)
            if b.ps:
                for k, v in b.r.items():
                    if k != e:
                        add((k, v))
        for b in writes:
            add(b.w)
            for k, v in b.r.items():
                add((k, v))
        for k, v in need.items():
            if k == e and (e == "pe" or (e != "dve" and not Sched.SIM)):
                continue
            self._wait(e, k, v)

    def _mark(self, tick, reads, writes):
        k, v = tick
        for b in reads:
            if b.r.get(k, 0) < v:
                b.r[k] = v
        for b in writes:
            b.w = tick
            b.r = {}

    def op(self, e, fn, reads=(), writes=()):
        self._deps(e, reads, writes)
        ins = fn(self.eng[e])
        self.cnt[e] += 1
        ins.then_inc(self.sems[e], 1)
        self._mark((e, self.cnt[e]), reads, writes)
        self.ninst += 1
        return ins

    def dma(self, q, out, in_, reads=(), writes=(), **kw):
        if q == "pool":
            return self.dma_pool(out, in_, reads, writes, wclass=kw.get("wclass", False))
        i = self.dnext
        self.dnext = (self.dnext + 1) % len(self.dsem)
        key = "d%d" % i
        self._wait(q, key, self.cnt[key])
        self._deps(q, reads, writes)
        ins = self.eng[q].dma_start(out=out, in_=in_, **kw)
        self.cnt[key] += 16
        ins.then_inc(self.dsem[i], 16)
        self._mark((key, self.cnt[key]), reads, writes)
        self.ninst += 1
        return ins

    def finish(self, bufs):
        for b in bufs:
            if b.w is not None:
                self._wait("sp", b.w[0], b.w[1])


D = 2048
DFF = 5632
NL = 2
SEQ = 2048
TT = 512
NSEQ_S = 4
TS = 16
WINS = (128, 512, 2048)
DILS = (1, 4, 16)
EPS = 1e-6
NEG = -30000.0
INW = 8192 + 1024
C_AQ, C_AK, C_AV, C_BU, C_BV, C_C, C_DQ, C_DK, C_DV, C_DG, C_DQS, C_DKS = (
    0, 1536, 3072, 4608, 5120, 5632, 6144, 6656, 7168, 7680, 8192, 8704)
TBL_OFF = (0, 256, 896)
TBL_W = 3072
ATT_SCALE = 128 ** -0.5
NWSLOT = 3


def _t5_buckets(dist):
    max_exact = 16
    d = np.maximum(dist, 1).astype(np.float32)
    large = max_exact + (np.log(d / max_exact) / np.log(2048 / max_exact) * (32 - max_exact)).astype(np.int32)
    large = np.minimum(large, 31)
    return np.where(dist < max_exact, dist, large).astype(np.int32)


def host_tables(rel_bias):
    tbl = np.full((4, 128, TBL_W), NEG, np.float32)
    q = np.arange(128)[:, None]
    for g in range(3):
        win, dil = WINS[g], DILS[g]
        u = np.arange(win + 128)[None, :]
        d = q + win - u
        valid = (d >= 0) & (d <= win) & (d % dil == 0)
        bk = _t5_buckets(np.clip(d, 0, None))
        for h in range(4):
            vals = rel_bias[bk, 4 * g + h]
            tbl[h, :, TBL_OFF[g]:TBL_OFF[g] + win + 128] = np.where(valid, vals, NEG)
    return tbl


def host_consts():
    c = {}
    half = 64
    inv = (10000.0 ** (-np.arange(half, dtype=np.float32) / half)).astype(np.float32)
    cc = np.zeros((5, 128, TT), np.float32)
    ss = np.zeros((5, 128, TT), np.float32)
    for p in range(5):
        if p < 4:
            pos = (TT * p + np.arange(TT)).astype(np.float32)
        else:
            pos = np.zeros(TT, np.float32)
            pos[:TS] = 8192 + (np.arange(TS) % 4)
        ang = pos[None, :] * np.concatenate([inv, inv])[:, None]
        cc[p] = np.cos(ang)
        sn = np.sin(ang)
        ss[p, :64] = -sn[:64]
        ss[p, 64:] = sn[64:]
    c["rot_cc"] = cc
    c["rot_ss"] = ss
    lg = np.log1p(-np.power(2.0, -5.0 - np.arange(4, dtype=np.float32))).astype(np.float32)
    for nm, cs in (("p", 128), ("s", 4)):
        i = np.arange(cs, dtype=np.float32)
        diff = i[:, None] - i[None, :]
        inner = np.where(diff >= 0, np.exp(np.maximum(diff, 0.0)[None] * lg[:, None, None]), 0.0)
        c["ret_dt_" + nm] = np.ascontiguousarray(inner.transpose(2, 0, 1)).astype(np.float32)
        qd = np.exp((i + 1.0)[None, :] * lg[:, None]).astype(np.float32)
        c["ret_qd_" + nm] = np.ascontiguousarray(np.broadcast_to(qd[None], (128, 4, cs))).astype(np.float32)
        kd = np.exp((cs - 1.0 - i)[None, :] * lg[:, None]).astype(np.float32)
        c["ret_kd_" + nm] = np.ascontiguousarray(kd.T).astype(np.float32)
        c["ret_cd_" + nm] = [float(np.exp(np.float32(cs) * lg[h])) for h in range(4)]
    ic = np.zeros((5, 128, 4, TT), np.float32)
    for p in range(5):
        pos = (TT * p + np.arange(TT)) if p < 4 else np.full(TT, 8192)
        for g, win in enumerate((2, 4, 8, 16)):
            ic[p, :, g, :] = (1.0 / np.minimum(pos + 1, win).astype(np.float32))[None, :]
    c["pool_ic"] = ic
    sw = np.zeros((128, 128), np.float32)
    c["ident"] = np.eye(128, dtype=np.float32)
    return c


def _prod(s):
    n = 1
    for v in s:
        n *= int(v)
    return n


class Prog:
    def __init__(self, passes=(0, 1, 2, 3, 4), nlayers=NL):
        self.nc = nc = bass.Bass("TRN2", target_bir_lowering=False)
        self.S = Sched(nc)
        self.passes = passes
        self.nlayers = nlayers
        self.cst = host_consts()
        self._dram()
        self._sbuf()
        self._emit()

    def _dram(self):
        nc = self.nc

        self.input_specs = {}

        def I(name, shape, dt=F32):
            self.input_specs[name] = tuple(shape)
            return nc.dram_tensor(name, list(shape), dt, kind="ExternalInput").ap()

        def O(name, shape):
            return nc.dram_tensor(name, list(shape), F32, kind="ExternalOutput").ap()

        self.xp = I("xp", [SEQ, D]); self.xs = I("xs", [TS, D])
        self.ckv = [I("ckv%d" % g, [NL, NSEQ_S, WINS[g], 2, 4, 128]) for g in range(3)]
        self.spool = I("spool", [NL, NSEQ_S, 15, 512]); self.sret = I("sret", [NL, NSEQ_S, 4, 128, 128])
        self.w_g1 = I("w_ffn1_gate", [NL, D, DFF]); self.w_u1 = I("w_ffn1_up", [NL, D, DFF]); self.w_d1 = I("w_ffn1_down", [NL, 16, 128, 44, 128])
        self.w_g2 = I("w_ffn2_gate", [NL, D, DFF]); self.w_u2 = I("w_ffn2_up", [NL, D, DFF]); self.w_d2 = I("w_ffn2_down", [NL, 16, 128, 44, 128])
        self.w_in = I("w_in", [NL, D, INW]); self.w_br = I("w_branch", [NL, 16, 128, 16, 128]); self.w_gate = I("w_gate", [NL, 16, 128, 64, 128]); self.w_out = I("w_out", [NL, D, D])
        self.gvec = I("gvec", [128, 7, 16])
        self.gsm = I("gsm", [128, NL, 2, 4])
        self.ggm = I("ggm", [NL, 512]); self.bsp = I("bsp", [NL, 512]); self.bsp_s = I("bsp_s", [NL, 4 * TS])
        self.wsT = I("wsT", [NL, 128, 4, 128]); self.wsT_s = I("wsT_s", [NL, TS, 4, TS])
        self.wpool = I("wpool", [NL, 128, 4, 128])
        self.tbl = I("tbl", [4, 128, TBL_W])
        self.rot_cc = I("rot_cc", [5, 128, TT]); self.rot_ss = I("rot_ss", [5, 128, TT])
        self.dt_p = I("ret_dt_p", [128, 4, 128]); self.qd_p = I("ret_qd_p", [128, 4, 128]); self.kd_p = I("ret_kd_p", [128, 4])
        self.dt_s = I("ret_dt_s", [4, 4, 4]); self.qd_s = I("ret_qd_s", [128, 4, 4]); self.kd_s = I("ret_kd_s", [4, 4])
        self.pool_ic = I("pool_ic", [5, 128, 4, TT]); self.ident_d = I("ident", [128, 128])
        self.y_p = O("y_p", [SEQ, D]); self.y_s = O("y_s", [TS, D])
        self.kv_p = [O("kv_p%d" % g, [NL, min(WINS[g], SEQ), 2, 4, 128]) for g in range(3)]
        self.kv_s = [O("kv_s%d" % g, [NL, NSEQ_S, 4, 2, 4, 128]) for g in range(3)]
        self.pool_p = O("pool_p", [NL, 15, 512]); self.pool_s = O("pool_s", [NL, NSEQ_S, 15, 512])
        self.ret_p = O("ret_p", [NL, 4, 128, 128]); self.ret_s = O("ret_s", [NL, NSEQ_S, 4, 128, 128])
        self.gv_s = O("gv_s", [NL, NSEQ_S, 4, 512])
        self.kT_hist = nc.dram_tensor("kT_hist", [NL, 3, 4, 128, SEQ], BF16, kind="Internal").ap()
        self.v_hist = nc.dram_tensor("v_hist", [NL, 3, SEQ, 512], BF16, kind="Internal").ap()
        self.b_kth = [[Buf("kth%d%d" % (l, g)) for g in range(3)] for l in range(NL)]
        self.b_vh = [[Buf("vh%d%d" % (l, g)) for g in range(3)] for l in range(NL)]

    def _sbuf(self):
        nc = self.nc
        NB = 204 * 1024
        self.arena = nc.alloc_sbuf_tensor("arena", [128, NB // 2], BF16)
        self.top = 0
        self.NB = NB

        def P(shape, dt):
            ap = self.view(self.top, shape, dt)
            self.top += _prod(shape) * (4 if dt == F32 else 2)
            self.top = (self.top + 63) // 64 * 64
            return ap
        self.xT = P([16, TT], F32); self.b_x = [Buf("x%d" % c) for c in range(16)]
        self.hT = P([16, TT], BF16); self.b_h = [Buf("h%d" % c) for c in range(16)]
        self.wslot = [P([8192], BF16) for _ in range(NWSLOT)]; self.b_w = [[Buf("w%d_%d" % (i, n)) for n in range(4)] for i in range(NWSLOT)]
        self.wnext = 0
        self.QT = P([12, TT], BF16); self.b_q = [Buf("q%d" % i) for i in range(12)]
        self.brT = P([16, TT], BF16); self.b_br = [Buf("br%d" % i) for i in range(16)]
        self.ident = P([128], F32); self.identb = P([128], BF16); self.onesb = P([128], BF16)
        self.b_const = Buf("const")
        self.gv = P([7, 16], F32); self.gs = P([NL, 2, 4], F32)
        self.ggm_b = P([NL, 512], F32); self.bsp_b = P([NL, 512], F32); self.bsps_b = P([NL, 4 * TS], F32)
        self.rstd = P([TT], F32); self.b_rstd = Buf("rstd")
        self.sq = [P([TT], BF16) for _ in range(2)]; self.b_sq = [Buf("sq0"), Buf("sq1")]
        self.sml = P([16], F32); self.b_sml = Buf("sml"); self.b_sml2 = Buf("sml2")
        self.Sst = P([NL, 4, 128], F32); self.Sbf = P([NL, 4, 128], BF16)
        self.b_S = [[Buf("S%d%d" % (l, h)) for h in range(4)] for l in range(NL)]
        self.b_Sb = [[Buf("Sb%d%d" % (l, h)) for h in range(4)] for l in range(NL)]
        self.halo = P([NL, 4, 15], F32); self.b_halo = [Buf("halo%d" % l) for l in range(NL)]
        self.dtp = P([4, 128], F32); self.qdp = P([4, 128], F32); self.kdp = P([4], F32)
        self.dts = P([4, 4], F32); self.qds = P([4, 4], F32); self.kds = P([4], F32)
        self.epsb = P([1], F32)
        self.ubase = self.top
        self.usize = NB - self.top
        print("SBUF persistent bytes", self.top, "U bytes", self.usize)
        self.ubufs = []
        self.ustate = {}
        self.utop = 0
        self.ps = [nc.alloc_psum_tensor("ps%d" % i, [128, 512], F32) for i in range(6)]
        self.b_ps = [Buf("ps%d" % i, ps=True) for i in range(6)]
        self.psn = 0
        self.pst = [nc.alloc_psum_tensor("pst%d" % i, [128, 1024], BF16) for i in range(2)]
        self.b_pst = [Buf("pst%d" % i, ps=True) for i in range(2)]
        self.pstn = 0

    def view(self, off, shape, dt):
        esz = 4 if dt == F32 else 2
        n = _prod(shape)
        assert off % 4 == 0 and off + n * esz <= self.NB, (off, shape, self.NB)
        ap = self.arena[:, off // 2: off // 2 + n * esz // 2]
        if dt == F32:
            ap = ap.bitcast(F32)
        if len(shape) == 2:
            ap = ap.rearrange("p (a b) -> p a b", b=shape[1])
        elif len(shape) == 3:
            ap = ap.rearrange("p (a b c) -> p a b c", b=shape[1], c=shape[2])
        return ap

    def u_phase(self):
        st = dict(self.ustate)
        for b in self.ubufs:
            if b.w is not None and st.get(b.w[0], 0) < b.w[1]:
                st[b.w[0]] = b.w[1]
            for k, v in b.r.items():
                if st.get(k, 0) < v:
                    st[k] = v
        self.ustate = st
        self.ubufs = []
        self.utop = 0

    def U(self, shape, dt, name="u"):
        ap = self.view(self.ubase + self.utop, shape, dt)
        self.utop += _prod(shape) * (4 if dt == F32 else 2)
        self.utop = (self.utop + 63) // 64 * 64
        assert self.utop <= self.usize, ("U overflow", self.utop, self.usize)
        b = Buf(name)
        b.r = dict(self.ustate)
        self.ubufs.append(b)
        return ap, b

    def getps(self):
        i = self.psn
        self.psn = (i + 1) % len(self.ps)
        return self.ps[i], self.b_ps[i]

    def getpst(self):
        i = self.pstn
        self.pstn = (i + 1) % len(self.pst)
        return self.pst[i], self.b_pst[i]

    def wload(self, src):
        i = self.wnext
        self.wnext = (i + 1) % NWSLOT
        shp = list(src.shape)
        n = _prod(shp[1:])
        assert n <= 8192
        dst = self.wslot[i][:, :n]
        if len(shp) == 3:
            dst = dst.rearrange("p (a b) -> p a b", b=shp[2])
        elif len(shp) == 4:
            dst = dst.rearrange("p (a b c) -> p a b c", b=shp[2], c=shp[3])
        if len(shp) == 4:
            for n in range(shp[2]):
                self.S.dma("pool", dst[:, :, n, :], src[:, :, n, :], writes=[self.b_w[i][n]], wclass=True)
        else:
            self.S.dma("pool", dst, src, writes=self.b_w[i], wclass=True)
        return dst, self.b_w[i]

    def gemm(self, W, c0, ncols, KC, srcs, T, fm=None, tm=None, tgroups=(), ci0=0, pre=None):
        S = self.S
        if pre is not None:
            w, wb = self.wload(pre)
        else:
            w, wb = self.wload(W[:, c0:c0 + ncols].rearrange("(kc p) n -> p kc n", p=128))
        if fm is not None:
            for cc in range(ncols // 128):
                ps, pb = self.getps()
                for kc in range(KC):
                    sa, sb = srcs[kc]
                    S.op("pe", lambda e, kc=kc, sa=sa, cc=cc, ps=ps: e.matmul(
                        ps[:, :T], w[:, kc, cc * 128:(cc + 1) * 128], sa[:, :T], start=(kc == 0), stop=(kc == KC - 1)),
                        reads=wb + [sb], writes=[pb])
                fm(ci0 + cc, ps[:, :T], pb)
        if tm is not None:
            for gi, (t0, M) in enumerate(tgroups):
                ps, pb = self.getps()
                for kc in range(KC):
                    sa, sb = srcs[kc]
                    S.op("pe", lambda e, kc=kc, sa=sa, ps=ps, t0=t0, M=M: e.matmul(
                        ps[:M, :ncols], sa[:, t0:t0 + M], w[:, kc, :], start=(kc == 0), stop=(kc == KC - 1)),
                        reads=wb + [sb], writes=[pb])
                tm(gi, ps[:M, :ncols], pb)

    def rmsnorm_fm(self, xs, T, gcol, outs, dim):
        S = self.S
        ps, pb = self.getps()
        n = len(xs)
        for c, (xa, xb) in enumerate(xs):
            sq, sqb = self.sq[c % 2], self.b_sq[c % 2]
            S.op("act", lambda e, xa=xa, sq=sq: e.activation(sq[:, :T], xa, AF.Square), reads=[xb], writes=[sqb])
            S.op("pe", lambda e, sq=sq, c=c: e.matmul(ps[:, :T], self.onesb[:, :], sq[:, :T], start=(c == 0), stop=(c == n - 1)),
                 reads=[sqb, self.b_const], writes=[pb])
        S.op("act", lambda e: e.activation(self.rstd[:, :T], ps[:, :T], AF.Sqrt, bias=self.epsb[:, 0:1], scale=1.0 / dim),
             reads=[pb, self.b_const], writes=[self.b_rstd])
        S.op("dve", lambda e: e.reciprocal(self.rstd[:, :T], self.rstd[:, :T]), reads=[self.b_rstd], writes=[self.b_rstd])
        for c, (xa, xb) in enumerate(xs):
            oa, ob = outs[c]
            S.op("dve", lambda e, xa=xa, oa=oa, c=c: e.scalar_tensor_tensor(
                oa, xa, gcol(c), self.rstd[:, :T], ALU.mult, ALU.mult), reads=[xb, self.b_rstd, self.b_const], writes=[ob])

    def ffn(self, l, which, T):
        S = self.S
        Wg, Wu, Wd = ((self.w_g1, self.w_u1, self.w_d1), (self.w_g2, self.w_u2, self.w_d2))[which]
        gi = (0 if which == 0 else 4) + l
        xs = [(self.xT[:, c, :T], self.b_x[c]) for c in range(16)]
        hs = [(self.hT[:, c, :T], self.b_h[c]) for c in range(16)]
        self.rmsnorm_fm(xs, T, lambda c: self.gv[:, gi, c:c + 1], hs, D)
        self.u_phase()
        hid, _ = self.U([44, TT], BF16, "hid")
        b_hid = [Buf("hid%d" % j) for j in range(44)]
        for b in b_hid:
            b.r = dict(self.ustate)
        self.ubufs.extend(b_hid)
        hsrc = [(self.hT[:, c, :], self.b_h[c]) for c in range(16)]
        for j0 in range(0, DFF, 512):
            def fg(ci, ps, pb):
                S.op("act", lambda e: e.activation(hid[:, ci, :T], ps, AF.Silu), reads=[pb], writes=[b_hid[ci]])

            def fu(ci, ps, pb):
                S.op("dve", lambda e: e.tensor_tensor(hid[:, ci, :T], hid[:, ci, :T], ps, ALU.mult), reads=[pb, b_hid[ci]], writes=[b_hid[ci]])
            self.gemm(Wg[l], j0, 512, 16, hsrc, T, fm=fg, ci0=j0 // 128)
            self.gemm(Wu[l], j0, 512, 16, hsrc, T, fm=fu, ci0=j0 // 128)
        hidsrc = [(hid[:, j, :], b_hid[j]) for j in range(44)]
        for d in range(16):
            def fd(ci, ps, pb):
                S.op("dve", lambda e: e.scalar_tensor_tensor(self.xT[:, ci, :T], ps, 0.5, self.xT[:, ci, :T], ALU.mult, ALU.add),
                     reads=[pb, self.b_x[ci]], writes=[self.b_x[ci]])
            self.gemm(None, 0, 128, 44, hidsrc, T, fm=fd, ci0=d, pre=Wd[l, d])

    def setup_consts(self):
        S = self.S
        b = self.b_const
        S.dma("sp", self.ident[:, :], self.ident_d, writes=[b])
        S.op("dve", lambda e: e.memset(self.onesb[:, :], 1.0), writes=[b])
        S.op("dve", lambda e: e.memset(self.epsb[:, :], EPS), writes=[b])
        S.op("dve", lambda e: e.tensor_copy(self.identb[:, :], self.ident[:, :]), reads=[b], writes=[b])
        zt, zb = self.U([1024], BF16, "zt")
        S.op("dve", lambda e: e.memset(zt[:, :], 0.0), writes=[zb])
        for i in range(len(self.pst)):
            for hh in range(8):
                S.op("pe", lambda e, i=i, hh=hh: e.transpose(self.pst[i][:, hh * 128:(hh + 1) * 128], zt[:, 0:128], self.identb[:, :]),
                     reads=[zb, b], writes=[self.b_pst[i]])
        S.dma("sp", self.gv[:, :, :], self.gvec, writes=[b])
        S.dma("sp", self.gs[:, :, :, :], self.gsm, writes=[b])
        for l in range(NL):
            S.dma("sp", self.ggm_b[:, l, :], self.ggm[l:l + 1, :].partition_broadcast(128), writes=[b])
            S.dma("sp", self.bsp_b[:, l, :], self.bsp[l:l + 1, :].partition_broadcast(128), writes=[b])
            S.dma("sp", self.bsps_b[:, l, :], self.bsp_s[l:l + 1, :].partition_broadcast(128), writes=[b])
        S.dma("sp", self.dtp[:, :, :], self.dt_p, writes=[b]); S.dma("sp", self.qdp[:, :, :], self.qd_p, writes=[b])
        S.dma("sp", self.kdp[:, :], self.kd_p, writes=[b])
        S.dma("sp", self.dts[:4, :, :4], self.dt_s, writes=[b]); S.dma("sp", self.qds[:, :, :4], self.qd_s, writes=[b])
        S.dma("sp", self.kds[:4, :], self.kd_s, writes=[b])

    def load_x(self, kind, T0, T):
        S = self.S
        self.u_phase()
        groups = [(cc * 128, 128) for cc in range(4)] if kind == "p" else [(0, TS)]
        for (t0, M) in groups:
            xin, xb = self.U([D], F32, "xin")
            src = self.xp[T0 + t0:T0 + t0 + M, :] if kind == "p" else self.xs
            S.dma("sp", xin[:M, :], src, writes=[xb])
            for c0 in range(0, 16, 4):
                ps, pb = self.getps()
                for j in range(4):
                    c = c0 + j
                    S.op("pe", lambda e, c=c, j=j, ps=ps, M=M, xin=xin: e.transpose(ps[:, j * 128:j * 128 + M], xin[:M, c * 128:(c + 1) * 128], self.ident[:M, :M]),
                         reads=[xb, self.b_const], writes=[pb])
                S.op("act", lambda e, c0=c0, ps=ps, t0=t0, M=M: e.copy(
                    self.xT[:, c0:c0 + 4, t0:t0 + M], ps[:, :].rearrange("p (j t) -> p j t", t=128)[:, :, :M]),
                    reads=[pb], writes=[self.b_x[c0 + j] for j in range(4)])

    def final_out(self, kind, T0, T):
        S = self.S
        xs = [(self.xT[:, c, :T], self.b_x[c]) for c in range(16)]
        self.u_phase()
        yT, b_y = self.xT, self.b_x
        outs = xs
        yos = [self.U([D], F32, "yo") for _ in range(2)]
        self.rmsnorm_fm(xs, T, lambda c: self.gv[:, 6, c:c + 1], outs, D)
        groups = [(cc * 128, 128) for cc in range(4)] if kind == "p" else [(0, TS)]
        for gi_, (t0, M) in enumerate(groups):
            yo, yob = yos[gi_ % 2]
            for c0 in range(0, 16, 4):
                ps, pb = self.getps()
                for j in range(4):
                    c = c0 + j
                    S.op("pe", lambda e, c=c, j=j, ps=ps, t0=t0, M=M: e.transpose(ps[:M, j * 128:(j + 1) * 128], yT[:, c, t0:t0 + M], self.ident[:, :]),
                         reads=[b_y[c], self.b_const], writes=[pb])
                S.op("act", lambda e, c0=c0, ps=ps, yo=yo, M=M: e.copy(yo[:M, c0 * 128:(c0 + 4) * 128], ps[:M, :]), reads=[pb], writes=[yob])
            dst = self.y_p[T0 + t0:T0 + t0 + M, :] if kind == "p" else self.y_s
            S.dma("sp", dst, yo[:M, :], reads=[yob])

    def _emit(self):
        S = self.S
        self.setup_consts()
        for p in self.passes:
            kind = "p" if p < 4 else "s"
            T0 = TT * p if p < 4 else 0
            T = TT if p < 4 else TS
            self.load_x(kind, T0, T)
            for l in range(self.nlayers):
                self.ffn(l, 0, T)
                if not getattr(self, "skip_mix", False):
                    self.mixers(l, p, kind, T0, T)
                if not getattr(self, "skip_ffn2", False):
                    self.ffn(l, 1, T)
            self.final_out(kind, T0, T)
        for i in range(len(S.sw)):
            S._wait("sp", "sw%d" % i, S.cnt["sw%d" % i])
        for i in range(len(S.dsem)):
            S._wait("sp", "d%d" % i, S.cnt["d%d" % i])

    def attn_core(self, M, qaps, segs, out_ap, out_buf, wk):
        S = self.S
        S_all, b_sa, Pn, b_pn, PT, b_pt, c0, bs_ = wk
        off = 0
        for (g, KT, kb, tb, tbb, vbl) in segs:
            ncols = KT.shape[1]
            qa, qb = qaps[g]
            for j0 in range(0, ncols, 512):
                n = min(512, ncols - j0)
                ps, pb = self.getps()
                S.op("pe", lambda e, ps=ps, qa=qa, KT=KT, j0=j0, n=n: e.matmul(ps[:M, :n], qa, KT[:, j0:j0 + n], start=True, stop=True),
                     reads=[qb, kb], writes=[pb])
                S.op("dve", lambda e, ps=ps, tb=tb, j0=j0, n=n, off=off: e.scalar_tensor_tensor(
                    S_all[:M, off + j0:off + j0 + n], ps[:M, :n], ATT_SCALE, tb[:, j0:j0 + n], ALU.mult, ALU.add),
                    reads=[pb, tbb], writes=[b_sa])
            off += ncols
        W = off
        sm, bs = self.sml, bs_
        S.op("dve", lambda e: e.reduce_max(sm[:M, c0:c0 + 1], S_all[:M, :W], AX.X), reads=[b_sa], writes=[bs])
        S.op("dve", lambda e: e.tensor_scalar(sm[:M, c0 + 1:c0 + 2], sm[:M, c0:c0 + 1], -1.0, None, ALU.mult), reads=[bs], writes=[bs])
        S.op("dve", lambda e: e.memset(sm[:M, c0 + 2:c0 + 3], 0.0), writes=[bs])
        S.op("act", lambda e: e.activation(S_all[:M, :W], S_all[:M, :W], AF.Exp, bias=sm[:M, c0 + 1:c0 + 2], scale=1.0, accum_out=sm[:M, c0 + 2:c0 + 3]),
             reads=[b_sa, bs], writes=[b_sa, bs])
        S.op("dve", lambda e: e.reciprocal(sm[:M, c0 + 3:c0 + 4], sm[:M, c0 + 2:c0 + 3]), reads=[bs], writes=[bs])
        S.op("dve", lambda e: e.tensor_scalar(Pn[:M, :W], S_all[:M, :W], sm[:M, c0 + 3:c0 + 4], None, ALU.mult), reads=[b_sa, bs], writes=[b_pn])
        blocks = []
        off = 0
        for (g, KT, kb, tb, tbb, vbl) in segs:
            c = off
            for (va, nk, vb) in vbl:
                blocks.append((c, nk, va, vb))
                c += nk
            off += KT.shape[1]
            assert c == off
        per = max(1, min(1024 // M, 32))
        for b0 in range(0, len(blocks), per):
            pt, ptb = self.getpst()
            nb = min(per, len(blocks) - b0)
            for j in range(nb):
                c, nk, va, vb = blocks[b0 + j]
                S.op("pe", lambda e, pt=pt, j=j, c=c, nk=nk: e.transpose(pt[:nk, j * M:(j + 1) * M], Pn[:M, c:c + nk], self.identb[:M, :M]),
                     reads=[b_pn, self.b_const], writes=[ptb])
            S.op("act", lambda e, pt=pt, b0=b0, nb=nb: e.copy(PT[:, b0 * M:(b0 + nb) * M], pt[:, :nb * M]), reads=[ptb], writes=[b_pt])
        ps, pb = self.getps()
        for bi, (c, nk, va, vb) in enumerate(blocks):
            S.op("pe", lambda e, bi=bi, nk=nk, va=va, ps=ps: e.matmul(ps[:, :M], va, PT[:nk, bi * M:(bi + 1) * M],
                                                                   start=(bi == 0), stop=(bi == len(blocks) - 1)),
                 reads=[vb, b_pt], writes=[pb])
        S.op("act", lambda e, ps=ps: e.copy(out_ap, ps[:, :M]), reads=[pb], writes=[out_buf])

    def mixers(self, l, p, kind, T0, T):
        S = self.S
        isp = kind == "p"
        xs = [(self.xT[:, c, :T], self.b_x[c]) for c in range(16)]
        hs = [(self.hT[:, c, :T], self.b_h[c]) for c in range(16)]
        self.rmsnorm_fm(xs, T, lambda c: self.gv[:, 2 + l, c:c + 1], hs, D)
        hsrc = [(self.hT[:, c, :], self.b_h[c]) for c in range(16)]
        Win = self.w_in[l]
        chunks = [(cc * 128, 128) for cc in range(4)] if isp else [(s * 4, 4) for s in range(4)]
        sm, bs = self.sml, self.b_sml
        bc = self.b_const
        brT, b_br = self.brT, self.b_br
        last_tile = isp and (T0 == SEQ - TT)

        SK = getattr(Prog, 'SKIP', set())
        if 'B' not in SK:
            self.u_phase()
            uT, b_u = self.U([4, TT], BF16, "uT")

            def fm_u(ci, ps, pb):
                S.op("act", lambda e: e.activation(uT[:, ci, :T], ps, AF.Gelu_apprx_tanh), reads=[pb], writes=[b_u])
            self.gemm(Win, C_BU, 512, 16, hsrc, T, fm=fm_u)
            vgroups = chunks if isp else [(0, TS)]
            vn, b_vn = self.U([len(vgroups), 512], BF16, "vn")
            gvt, b_gvt = self.U([512], F32, "gvt")
            junk, b_junk = self.U([512], F32, "junk")
            vnf, b_vnf = self.U([512], F32, "vnf")

            def tm_v(gi, ps, pb):
                M = vgroups[gi][1]
                S.op("act", lambda e: e.activation(gvt[:M, :], ps, AF.Gelu_apprx_tanh), reads=[pb], writes=[b_gvt])
                S.op("dve", lambda e: e.memset(sm[:M, 0:1], 0.0), writes=[bs])
                S.op("act", lambda e: e.activation(junk[:M, :], gvt[:M, :], AF.Square, accum_out=sm[:M, 0:1]), reads=[b_gvt, bs], writes=[b_junk, bs])
                S.op("act", lambda e: e.activation(sm[:M, 1:2], sm[:M, 0:1], AF.Sqrt, bias=self.epsb[:M, 0:1], scale=1.0 / 512), reads=[bs, bc], writes=[bs])
                S.op("dve", lambda e: e.reciprocal(sm[:M, 1:2], sm[:M, 1:2]), reads=[bs], writes=[bs])
                S.op("dve", lambda e: e.scalar_tensor_tensor(vnf[:M, :], gvt[:M, :], sm[:M, 1:2], self.ggm_b[:M, l, :], ALU.mult, ALU.mult),
                     reads=[b_gvt, bs, bc], writes=[b_vnf])
                S.op("act", lambda e: e.copy(vn[:M, gi, :], vnf[:M, :]), reads=[b_vnf], writes=[b_vn])
                if not isp:
                    S.dma("sp", self.gv_s[l].rearrange("s t c -> (s t) c"), vnf[:TS, :], reads=[b_vnf])
            self.gemm(Win, C_BV, 512, 16, hsrc, T, tm=tm_v, tgroups=vgroups)
            wst, b_wst = self.U([4, 128], BF16, "wst")
            if isp:
                S.dma("pool", wst[:, :, :], self.wsT[l], writes=[b_wst])
            else:
                S.dma("pool", wst[:TS, :, :TS], self.wsT_s[l], writes=[b_wst])
            t1, b_t1 = self.U([128], F32, "t1")
            for gi, (t0, M) in enumerate(vgroups):
                for g in range(4):
                    ps, pb = self.getps()
                    S.op("pe", lambda e, ps=ps, gi=gi, g=g, M=M: e.matmul(ps[:, :M], vn[:M, gi, g * 128:(g + 1) * 128], wst[:M, g, :M], start=True, stop=True),
                         reads=[b_vn, b_wst], writes=[pb])
                    bias = self.bsp_b[:, l, g * 128:g * 128 + M] if isp else self.bsps_b[:, l, g * TS:(g + 1) * TS]
                    S.op("dve", lambda e, ps=ps, bias=bias, M=M: e.tensor_tensor(t1[:, :M], ps[:, :M], bias, ALU.add), reads=[pb, bc], writes=[b_t1])
                    S.op("dve", lambda e, g=g, t0=t0, M=M: e.tensor_tensor(brT[:, 4 + g, t0:t0 + M], t1[:, :M], uT[:, g, t0:t0 + M], ALU.mult),
                         reads=[b_t1, b_u], writes=[b_br[4 + g]])

        if 'C' not in SK:
            self.u_phase()
            NSQ = 1 if isp else 4
            Tn = T if isp else 4
            L = 15 + Tn
            cin, b_cin = self.U([4, NSQ * L], F32, "cin")
            cin4 = cin.rearrange("p g (s j) -> p g s j", j=L)
            if isp:
                if T0 == 0:
                    S.op("dve", lambda e: e.memset(cin4[:, :, 0, 0:15], 0.0), writes=[b_cin])
                else:
                    S.op("dve", lambda e: e.tensor_copy(cin4[:, :, 0, 0:15], self.halo[:, l, :, :]), reads=[self.b_halo[l]], writes=[b_cin])
            else:
                spin, b_spin = self.U([NSEQ_S, 512], F32, "spin")
                S.dma("sp", spin[:15, :, :], self.spool[l].rearrange("s r c -> r s c"), writes=[b_spin])
                for g in range(4):
                    ps, pb = self.getps()
                    for s in range(4):
                        S.op("pe", lambda e, ps=ps, s=s, g=g: e.transpose(ps[:, s * 16:s * 16 + 15], spin[:15, s, g * 128:(g + 1) * 128], self.ident[:15, :15]),
                             reads=[b_spin, bc], writes=[pb])
                    S.op("act", lambda e, ps=ps, g=g: e.copy(cin4[:, g, :, 0:15], ps[:, 0:64].rearrange("p (s j) -> p s j", j=16)[:, :, 0:15]),
                         reads=[pb], writes=[b_cin])
                S.dma("sp", self.pool_s[l][:, 0:11, :], self.spool[l][:, 4:15, :])

            def fm_c(ci, ps, pb):
                if isp:
                    S.op("act", lambda e: e.copy(cin4[:, ci, 0, 15:15 + T], ps), reads=[pb], writes=[b_cin])
                else:
                    S.op("act", lambda e: e.copy(cin4[:, ci, :, 15:19], ps.rearrange("p (s t) -> p s t", t=4)), reads=[pb], writes=[b_cin])
            ctm, b_ctm = self.U([512], F32, "ctm")
            ctg = [(384, 128)] if isp else chunks

            def tm_c(gi, ps, pb):
                M = ctg[gi][1]
                S.op("act", lambda e: e.copy(ctm[:M, :], ps), reads=[pb], writes=[b_ctm])
                if isp:
                    S.dma("sp", self.pool_p[l], ctm[113:128, :], reads=[b_ctm])
                else:
                    S.dma("sp", self.pool_s[l][gi, 11:15, :], ctm[:4, :], reads=[b_ctm])
            self.gemm(Win, C_C, 512, 16, hsrc, T, fm=fm_c, tm=(tm_c if (last_tile or not isp) else None), tgroups=ctg)
            wa, b_wa = self.U([NSQ * L], F32, "wa")
            wb_, b_wb = self.U([NSQ * L], F32, "wb")
            S.op("dve", lambda e: e.memset(wa[:, :], 0.0), writes=[b_wa])
            S.op("dve", lambda e: e.memset(wb_[:, :], 0.0), writes=[b_wb])
            wa3 = wa.rearrange("p (s j) -> p s j", j=L)
            wb3 = wb_.rearrange("p (s j) -> p s j", j=L)
            dif, b_dif = self.U([TT], BF16, "dif")
            tmpf, b_tmpf = self.U([TT], F32, "tmpf")
            ict, b_ict = self.U([4, TT], F32, "ict")
            S.dma("sp", ict[:, :, :], self.pool_ic[p], writes=[b_ict])
            wpl, b_wpl = self.U([4, 128], BF16, "wpl")
            S.dma("pool", wpl[:, :, :], self.wpool[l], writes=[b_wpl])
            for g in range(4):
                cur, curb = cin4[:, g], b_cin
                pp = [(wa3, b_wa), (wb3, b_wb)]
                for k in range(g + 1):
                    sh = 1 << k
                    dst, dstb = pp[k % 2]
                    S.op("dve", lambda e, dst=dst, cur=cur, sh=sh: e.tensor_tensor(dst[:, :, sh:L], cur[:, :, sh:L], cur[:, :, 0:L - sh], ALU.add),
                         reads=[curb], writes=[dstb])
                    cur, curb = dst, dstb
                if isp:
                    tcur = cur[:, 0, 15:L]; xcur = cin4[:, g, 0, 15:L]; icv = ict[:, g, :T]; tf = tmpf[:, :T]; df = dif[:, :T]
                else:
                    tcur = cur[:, :, 15:L]; xcur = cin4[:, g, :, 15:L]
                    icv = ict[:, g, 0:TS].rearrange("p (s t) -> p s t", t=4)
                    tf = tmpf[:, :TS].rearrange("p (s t) -> p s t", t=4); df = dif[:, :TS].rearrange("p (s t) -> p s t", t=4)
                S.op("dve", lambda e, tf=tf, tcur=tcur, icv=icv: e.tensor_tensor(tf, tcur, icv, ALU.mult), reads=[curb, b_ict], writes=[b_tmpf])
                S.op("dve", lambda e, tf=tf, df=df, xcur=xcur: e.tensor_tensor(df, tf, xcur, ALU.subtract), reads=[b_tmpf, b_cin], writes=[b_dif])
                ps, pb = self.getps()
                S.op("pe", lambda e, ps=ps, g=g: e.matmul(ps[:, :T], wpl[:, g, :], dif[:, :T], start=True, stop=True), reads=[b_wpl, b_dif], writes=[pb])
                S.op("act", lambda e, ps=ps, g=g: e.mul(brT[:, 8 + g, :T], ps[:, :T], self.gs[:, l, 0, g:g + 1]), reads=[pb, bc], writes=[b_br[8 + g]])
            if isp:
                S.op("dve", lambda e: e.tensor_copy(self.halo[:, l, :, :], cin4[:, :, 0, L - 15:L]), reads=[b_cin], writes=[self.b_halo[l]])

        if 'D' not in SK:
            self.u_phase()
            cc_t, b_cc = self.U([TT], F32, "cc")
            ss_t, b_ss = self.U([TT], F32, "ss")
            S.dma("sp", cc_t[:, :], self.rot_cc[p], writes=[b_cc])
            S.dma("sp", ss_t[:, :], self.rot_ss[p], writes=[b_ss])
            qa, b_qa = self.U([4, TT], F32, "qa")
            rq, b_rq = self.U([4, TT], BF16, "rq")
            rk, b_rk = self.U([4, TT], BF16, "rk")
            r1, b_r1 = self.U([TT], F32, "r1")
            r2, b_r2 = self.U([TT], F32, "r2")
            for (cA, cS, dst, b_dst) in ((C_DQ, C_DQS, rq, b_rq), (C_DK, C_DKS, rk, b_rk)):
                def fm_a(ci, ps, pb):
                    S.op("act", lambda e: e.copy(qa[:, ci, :T], ps), reads=[pb], writes=[b_qa])
                self.gemm(Win, cA, 512, 16, hsrc, T, fm=fm_a)

                def fm_s(ci, ps, pb, dst=dst, b_dst=b_dst):
                    S.op("dve", lambda e: e.tensor_tensor(r1[:, :T], qa[:, ci, :T], cc_t[:, :T], ALU.mult), reads=[b_qa, b_cc], writes=[b_r1])
                    S.op("dve", lambda e: e.tensor_tensor(r2[:, :T], ps, ss_t[:, :T], ALU.mult), reads=[pb, b_ss], writes=[b_r2])
                    S.op("dve", lambda e: e.tensor_tensor(dst[:, ci, :T], r1[:, :T], r2[:, :T], ALU.add), reads=[b_r1, b_r2], writes=[b_dst])
                self.gemm(Win, cS, 512, 16, hsrc, T, fm=fm_s)
            vt, b_vt = self.U([len(chunks), 512], BF16, "vt")

            def tm_dv(gi, ps, pb):
                M = chunks[gi][1]
                S.op("act", lambda e: e.mul(vt[:M, gi, :], ps, ATT_SCALE), reads=[pb], writes=[b_vt])
            self.gemm(Win, C_DV, 512, 16, hsrc, T, tm=tm_dv, tgroups=chunks)
            sg, b_sg = self.U([4, TT], BF16, "sg")

            def fm_g(ci, ps, pb):
                S.op("act", lambda e: e.activation(sg[:, ci, :T], ps, AF.Silu), reads=[pb], writes=[b_sg])
            self.gemm(Win, C_DG, 512, 16, hsrc, T, fm=fm_g)
            oT, b_oT = self.U([4, TT], F32, "oT")
            rtmp = [[self.U([128], BF16, "rt%d%d" % (h_, j_)) for j_ in range(4)] for h_ in range(4)]
            if isp:
                dt_, qd_, kd_, cd = self.dtp, self.qdp, self.kdp, self.cst["ret_cd_p"]
            else:
                dt_, qd_, kd_, cd = self.dts, self.qds, self.kds, self.cst["ret_cd_s"]
                Ssms = [self.U([128], F32, "Ssm%d" % h_) for h_ in range(4)]
                Ssbs = [self.U([128], BF16, "Ssb%d" % h_) for h_ in range(4)]
            for gi, (t0, M) in enumerate(chunks):
                for h in range(4):
                    (atd, b_atd), (rqd, b_rqd), (rkt, b_rkt), (vk, b_vk) = rtmp[h]
                    if isp:
                        Sf, Sb_, bS, bSb = self.Sst[:, l, h, :], self.Sbf[:, l, h, :], self.b_S[l][h], self.b_Sb[l][h]
                        if T0 == 0 and gi == 0:
                            S.op("dve", lambda e, Sf=Sf: e.memset(Sf, 0.0), writes=[bS])
                            S.op("dve", lambda e, Sb_=Sb_: e.memset(Sb_, 0.0), writes=[bSb])
                    else:
                        Sf, Sb_, bS, bSb = Ssms[h][0][:, :], Ssbs[h][0][:, :], Ssms[h][1], Ssbs[h][1]
                        S.dma("sp", Sf, self.sret[l, gi, h], writes=[bS])
                        S.op("act", lambda e, Sf=Sf, Sb_=Sb_: e.copy(Sb_, Sf), reads=[bS], writes=[bSb])
                    ps, pb = self.getps()
                    S.op("pe", lambda e, ps=ps, h=h, t0=t0, M=M: e.matmul(ps[:M, :M], rk[:, h, t0:t0 + M], rq[:, h, t0:t0 + M], start=True, stop=True),
                         reads=[b_rk, b_rq], writes=[pb])
                    S.op("dve", lambda e, ps=ps, h=h, M=M: e.tensor_tensor(atd[:M, :M], ps[:M, :M], dt_[:M, h, :M], ALU.mult), reads=[pb, bc], writes=[b_atd])
                    S.op("dve", lambda e, h=h, t0=t0, M=M: e.tensor_tensor(rqd[:, :M], rq[:, h, t0:t0 + M], qd_[:, h, :M], ALU.mult), reads=[b_rq, bc], writes=[b_rqd])
                    ps2, pb2 = self.getps()
                    S.op("pe", lambda e, ps2=ps2, gi=gi, h=h, M=M: e.matmul(ps2[:, :M], vt[:M, gi, h * 128:(h + 1) * 128], atd[:M, :M], start=True, stop=False),
                         reads=[b_vt, b_atd], writes=[pb2])
                    S.op("pe", lambda e, ps2=ps2, Sb_=Sb_, M=M: e.matmul(ps2[:, :M], Sb_, rqd[:, :M], start=False, stop=True), reads=[bSb, b_rqd], writes=[pb2])
                    S.op("act", lambda e, ps2=ps2, h=h, t0=t0, M=M: e.copy(oT[:, h, t0:t0 + M], ps2[:, :M]), reads=[pb2], writes=[b_oT])
                    pt, ptb = self.getpst()
                    S.op("pe", lambda e, pt=pt, h=h, t0=t0, M=M: e.transpose(pt[:M, :128], rk[:, h, t0:t0 + M], self.identb[:, :]), reads=[b_rk, bc], writes=[ptb])
                    S.op("act", lambda e, pt=pt, M=M: e.copy(rkt[:M, :], pt[:M, :128]), reads=[ptb], writes=[b_rkt])
                    S.op("dve", lambda e, gi=gi, h=h, M=M: e.tensor_scalar(vk[:M, :], vt[:M, gi, h * 128:(h + 1) * 128], kd_[:M, h:h + 1], None, ALU.mult),
                         reads=[b_vt, bc], writes=[b_vk])
                    ps3, pb3 = self.getps()
                    S.op("pe", lambda e, ps3=ps3, M=M: e.matmul(ps3[:, :128], rkt[:M, :], vk[:M, :], start=True, stop=True), reads=[b_rkt, b_vk], writes=[pb3])
                    S.op("dve", lambda e, ps3=ps3, Sf=Sf, h=h: e.scalar_tensor_tensor(Sf, Sf, cd[h], ps3[:, :128], ALU.mult, ALU.add), reads=[bS, pb3], writes=[bS])
                    if isp:
                        S.op("act", lambda e, Sf=Sf, Sb_=Sb_: e.copy(Sb_, Sf), reads=[bS], writes=[bSb])
                        if last_tile and gi == 3:
                            S.dma("sp", self.ret_p[l, h], Sf, reads=[bS])
                    else:
                        S.dma("sp", self.ret_s[l, gi, h], Sf, reads=[bS])
            for h in range(4):
                sq, sqb = self.sq[h % 2], self.b_sq[h % 2]
                S.op("act", lambda e, sq=sq, h=h: e.activation(sq[:, :T], oT[:, h, :T], AF.Square), reads=[b_oT], writes=[sqb])
                ps, pb = self.getps()
                S.op("pe", lambda e, ps=ps, sq=sq: e.matmul(ps[:, :T], self.onesb[:, :], sq[:, :T], start=True, stop=True), reads=[sqb, bc], writes=[pb])
                S.op("act", lambda e, ps=ps: e.activation(self.rstd[:, :T], ps[:, :T], AF.Sqrt, bias=self.epsb[:, 0:1], scale=1.0 / 128),
                     reads=[pb, bc], writes=[self.b_rstd])
                S.op("dve", lambda e: e.reciprocal(self.rstd[:, :T], self.rstd[:, :T]), reads=[self.b_rstd], writes=[self.b_rstd])
                S.op("dve", lambda e, h=h: e.scalar_tensor_tensor(r1[:, :T], oT[:, h, :T], self.gs[:, l, 1, h:h + 1], self.rstd[:, :T], ALU.mult, ALU.mult),
                     reads=[b_oT, self.b_rstd, bc], writes=[b_r1])
                S.op("dve", lambda e, h=h: e.tensor_tensor(brT[:, 12 + h, :T], r1[:, :T], sg[:, h, :T], ALU.mult), reads=[b_r1, b_sg], writes=[b_br[12 + h]])

        if 'A' not in SK:
            self.u_phase()
            QT, b_q = self.QT, self.b_q

            def fm_q(ci, ps, pb):
                S.op("act", lambda e: e.copy(QT[:, ci, :T], ps), reads=[pb], writes=[b_q[ci]])
            AQ = getattr(Prog, "AQ", 15)
            for g in range(3):
                if AQ & 1:
                    self.gemm(Win, C_AQ + g * 512, 512, 16, hsrc, T, fm=fm_q, ci0=g * 4)
            kfs = [self.U([512], F32, "kf%d" % i) for i in range(2)]
            kfn = [0]
            if isp:
                kts = [self.U([TT], BF16, "kt%d" % i) for i in range(2)]
                vbs = [self.U([512], BF16, "vb%d" % i) for i in range(2)]
            else:
                KTn, b_KTn = self.U([12, TS], BF16, "KTn")
                Vn, b_Vn = self.U([NSEQ_S, 1536], BF16, "Vn")
            for g in range(3):
                keep = min(WINS[g], SEQ)

                def fm_k(ci, ps, pb, g=g):
                    h = ci - 4 * g
                    if isp:
                        kt, ktb = kts[ci % 2]
                        S.op("act", lambda e: e.copy(kt[:, :T], ps), reads=[pb], writes=[ktb])
                        S.dma("sp", self.kT_hist[l, g, h][:, T0:T0 + T], kt[:, :T], reads=[ktb], writes=[self.b_kth[l][g]])
                    else:
                        S.op("act", lambda e: e.copy(KTn[:, ci, :T], ps), reads=[pb], writes=[b_KTn if not getattr(Prog, "HYP", 0) else Buf("tmpk")])
                if isp:
                    ktg = [(t0, M) for (t0, M) in chunks if T0 + t0 >= SEQ - keep]
                else:
                    ktg = chunks

                def tm_kv(gi, ps, pb, g=g, which=0, grp=None):
                    t0, M = grp[gi]
                    kf, kfb = kfs[kfn[0] % 2]
                    kfn[0] += 1
                    S.op("act", lambda e: e.copy(kf[:M, :], ps), reads=[pb], writes=[kfb])
                    if isp:
                        r0 = T0 + t0 - (SEQ - keep)
                        if r0 >= 0:
                            S.dma("sp", self.kv_p[g][l, r0:r0 + M, which].rearrange("r h d -> r (h d)"), kf[:M, :], reads=[kfb])
                    else:
                        S.dma("sp", self.kv_s[g][l, gi, :, which].rearrange("t h d -> t (h d)"), kf[:M, :], reads=[kfb])
                if AQ & 6:
                    self.gemm(Win, C_AK + g * 512, 512, 16, hsrc, T, fm=(fm_k if AQ & 2 else None),
                              tm=((lambda gi, ps, pb, g=g, ktg=ktg: tm_kv(gi, ps, pb, g=g, which=0, grp=ktg)) if (ktg and (AQ & 4)) else None),
                              tgroups=ktg, ci0=g * 4)

                def tm_v(gi, ps, pb, g=g):
                    t0, M = chunks[gi]
                    if (not isp) or (T0 + t0 >= SEQ - keep):
                        tm_kv(gi, ps, pb, g=g, which=1, grp=chunks)
                    if isp:
                        vb, vbb = vbs[gi % 2]
                        S.op("dve", lambda e: e.tensor_copy(vb[:M, :], ps), reads=[pb], writes=[vbb])
                        S.dma("sp", self.v_hist[l, g][T0 + t0:T0 + t0 + M, :], vb[:M, :], reads=[vbb], writes=[self.b_vh[l][g]])
                    else:
                        if getattr(Prog, "HYP", 0) != 2:
                            S.op("dve", lambda e: e.tensor_copy(Vn[:M, gi, g * 512:(g + 1) * 512], ps), reads=[pb], writes=[b_Vn])
                if AQ & 8:
                    self.gemm(Win, C_AV + g * 512, 512, 16, hsrc, T, tm=tm_v, tgroups=chunks)

            AST = getattr(Prog, "AST", 3)
            allw = [b_ for i_ in range(NWSLOT) for b_ in self.b_w[i_]]
            S.op("dve", lambda e: e.memset(self.sml[:1, 15:16], 0.0), writes=allw)
            if AST < 2:
                pass
            elif isp:
                self.u_phase()
                KTw, b_KTw = self.U([3712], BF16, "KTw")
                Vw, b_Vw = self.U([29, 128], BF16, "Vw")
                tb, b_tb = self.U([TBL_W], BF16, "tb")
                wkA = self.U([TBL_W], F32, "S_all") + self.U([TBL_W], BF16, "Pn") + self.U([TBL_W], BF16, "PT") + (2, self.b_sml)
                wkB = (self.wslot[0][:, :2 * TBL_W].bitcast(F32), self.b_w[0][0], self.wslot[1][:, :TBL_W], self.b_w[1][0],
                       self.wslot[1][:, TBL_W:2 * TBL_W], self.b_w[1][1], 8, self.b_sml2)
                wks = [wkA, wkB]
                unit = [0]
                KVs = [(KTw, b_KTw, Vw, b_Vw),
                       (self.wslot[2][:, :3712], self.b_w[2][0], self.wslot[2][:, 3712:3712 + 29 * 128].rearrange("p (c d) -> p c d", d=128), self.b_w[2][1])]
                koff = (0, 640, 1664)
                voff = (0, 5, 13)
                for h in range(4):
                    KTw, b_KTw, Vw, b_Vw = KVs[h % 2]
                    S.dma("pool", tb[:, :], self.tbl[h], writes=[b_tb])
                    los = []
                    for g in range(3):
                        lo = max(0, T0 - WINS[g])
                        los.append(lo)
                        n = T0 + TT - lo
                        S.dma("sp", KTw[:, koff[g]:koff[g] + n], self.kT_hist[l, g, h][:, lo:T0 + TT], reads=[self.b_kth[l][g]], writes=[b_KTw])
                        S.dma("sp", Vw[:, voff[g]:voff[g] + n // 128, :],
                              self.v_hist[l, g][lo:T0 + TT, h * 128:(h + 1) * 128].rearrange("(c p) d -> p c d", p=128),
                              reads=[self.b_vh[l][g]], writes=[b_Vw])
                    for (t0, M) in chunks:
                        P0 = T0 + t0
                        segs = []
                        for g in range(3):
                            win = WINS[g]
                            lo_c = max(0, P0 - win)
                            ncols = P0 + 128 - lo_c
                            ks = koff[g] + (lo_c - los[g])
                            tcol = TBL_OFF[g] + (win + 128 - ncols)
                            vbl = [(Vw[:, voff[g] + (lo_c - los[g]) // 128 + j, :], 128, b_Vw) for j in range(ncols // 128)]
                            segs.append((g, KTw[:, ks:ks + ncols], b_KTw, tb[:, tcol:tcol + ncols], b_tb, vbl))
                        qaps = [(QT[:, g * 4 + h, t0:t0 + M], b_q[g * 4 + h]) for g in range(3)]
                        if AST >= 3:
                            self.attn_core(M, qaps, segs, brT[:, h, t0:t0 + M], b_br[h], wks[unit[0] % 2])
                            unit[0] += 1
            else:
                Kc, b_Kc = self.U([16, 128], BF16, "Kc")
                Vw, b_Vw = self.U([21, 128], BF16, "Vw")
                KTw, b_KTw = self.U([2704], BF16, "KTw")
                tb, b_tb = self.U([TBL_W], BF16, "tb")
                wkA = self.U([2704], F32, "S_all") + self.U([2704], BF16, "Pn") + self.U([128], BF16, "PT") + (2, self.b_sml)
                wkB = (self.wslot[0][:, :2 * 2704].bitcast(F32), self.b_w[0][0], self.wslot[1][:, :2704], self.b_w[1][0],
                       self.wslot[1][:, 2704:2704 + 128], self.b_w[1][1], 8, self.b_sml2)
                wks = [wkA, wkB]
                unit = [0]
                koff = (0, 132, 648)
                voff = (0, 1, 5)
                for h in range(4):
                    S.dma("pool", tb[:, :], self.tbl[h], writes=[b_tb])
                    for (t0, M) in chunks:
                        s = t0 // 4
                        segs = []
                        for g in range(3):
                            win = WINS[g]
                            nch = win // 128
                            S.dma("pool", Kc[:, :nch, :], self.ckv[g][l, s, :, 0, h, :].rearrange("(c p) d -> p c d", p=128), writes=[b_Kc])
                            S.dma("pool", Vw[:, voff[g]:voff[g] + nch, :], self.ckv[g][l, s, :, 1, h, :].rearrange("(c p) d -> p c d", p=128), writes=[b_Vw])
                            for j0 in range(0, nch, 8):
                                pt, ptb = self.getpst()
                                nb = min(8, nch - j0)
                                for j in range(nb):
                                    S.op("pe", lambda e, pt=pt, j=j, j0=j0: e.transpose(pt[:, j * 128:(j + 1) * 128], Kc[:, j0 + j, :], self.identb[:, :]),
                                         reads=[b_Kc, bc], writes=[ptb])
                                S.op("act", lambda e, pt=pt, j0=j0, nb=nb, g=g: e.copy(KTw[:, koff[g] + j0 * 128:koff[g] + (j0 + nb) * 128], pt[:, :nb * 128]),
                                     reads=[ptb], writes=[b_KTw])
                            S.op("act", lambda e, g=g, t0=t0: e.copy(KTw[:, koff[g] + win:koff[g] + win + 4], KTn[:, g * 4 + h, t0:t0 + 4]),
                                 reads=[b_KTn], writes=[b_KTw])
                            vbl = [(Vw[:, voff[g] + j, :], 128, b_Vw) for j in range(nch)]
                            vbl.append((Vn[:4, s, g * 512 + h * 128:g * 512 + (h + 1) * 128], 4, b_Vn))
                            segs.append((g, KTw[:, koff[g]:koff[g] + win + 4], b_KTw, tb[:M, TBL_OFF[g]:TBL_OFF[g] + win + 4], b_tb, vbl))
                        qaps = [(QT[:, g * 4 + h, t0:t0 + M], b_q[g * 4 + h]) for g in range(3)]
                        if AST >= 3:
                            self.attn_core(M, qaps, segs, brT[:, h, t0:t0 + M], b_br[h], wks[unit[0] % 2])
                            unit[0] += 1

        if 'A' not in SK:
            S.op("dve", lambda e: e.memset(self.sml[:1, 15:16], 0.0), writes=allw)
        if 'M' not in SK:
            self.u_phase()
            mg, _ = self.U([16, TT], BF16, "mg")
            b_mg = [Buf("mg%d" % i) for i in range(16)]
            for b in b_mg:
                b.r = dict(self.ustate)
            self.ubufs.extend(b_mg)
            acc, b_acc = self.U([TT], F32, "acc")
            sgt, b_sgt = self.U([TT], F32, "sgt")
            tt, b_tt = self.U([TT], F32, "tt")
            for dc in range(16):
                wg_, wgb = self.wload(self.w_gate[l, dc])
                wb2, wbb = self.wload(self.w_br[l, dc])
                for n in range(4):
                    psg, pgb = self.getps()
                    for kc in range(16):
                        S.op("pe", lambda e, psg=psg, kc=kc, n=n: e.matmul(psg[:, :T], wg_[:, kc * 4 + n, :], self.hT[:, kc, :T], start=(kc == 0), stop=(kc == 15)),
                             reads=wgb + [self.b_h[kc]], writes=[pgb])
                    psp, ppb = self.getps()
                    for kc in range(4):
                        S.op("pe", lambda e, psp=psp, kc=kc, n=n: e.matmul(psp[:, :T], wb2[:, 4 * n + kc, :], brT[:, 4 * n + kc, :T], start=(kc == 0), stop=(kc == 3)),
                             reads=wbb + [b_br[4 * n + kc]], writes=[ppb])
                    S.op("act", lambda e, psg=psg: e.activation(sgt[:, :T], psg[:, :T], AF.Sigmoid), reads=[pgb], writes=[b_sgt])
                    if n == 0:
                        S.op("dve", lambda e, psp=psp: e.tensor_tensor(acc[:, :T], sgt[:, :T], psp[:, :T], ALU.mult), reads=[b_sgt, ppb], writes=[b_acc])
                    else:
                        S.op("dve", lambda e, psp=psp: e.tensor_tensor(tt[:, :T], sgt[:, :T], psp[:, :T], ALU.mult), reads=[b_sgt, ppb], writes=[b_tt])
                        if n < 3:
                            S.op("dve", lambda e: e.tensor_tensor(acc[:, :T], acc[:, :T], tt[:, :T], ALU.add), reads=[b_acc, b_tt], writes=[b_acc])
                        else:
                            S.op("dve", lambda e, dc=dc: e.tensor_tensor(mg[:, dc, :T], acc[:, :T], tt[:, :T], ALU.add), reads=[b_acc, b_tt], writes=[b_mg[dc]])
            mgsrc = [(mg[:, kc, :], b_mg[kc]) for kc in range(16)]

            def fo(ci, ps, pb):
                S.op("dve", lambda e: e.tensor_tensor(self.xT[:, ci, :T], self.xT[:, ci, :T], ps, ALU.add), reads=[pb, self.b_x[ci]], writes=[self.b_x[ci]])
            for j0 in range(0, D, 512):
                self.gemm(self.w_out[l], j0, 512, 16, mgsrc, T, fm=fo, ci0=j0 // 128)


_PROG = None


def _layout_inputs(inp):
    f = lambda a: np.ascontiguousarray(np.asarray(a, dtype=np.float32))
    w_in = f(inp["w_in"])
    perm = np.concatenate([h * 128 + (np.arange(128) + 64) % 128 for h in range(4)])
    w_in_ext = np.concatenate([w_in[:, :, :8192], w_in[:, :, C_DQ + perm], w_in[:, :, C_DK + perm]], axis=2)
    w_gate = np.ascontiguousarray(w_in[:, :, 8192:16384].reshape(NL, 16, 128, 4, 16, 128).transpose(0, 4, 2, 1, 3, 5)).reshape(NL, 16, 128, 64, 128)

    def tile_down(a):
        return np.ascontiguousarray(f(a).reshape(NL, 44, 128, 16, 128).transpose(0, 3, 2, 1, 4))
    shared = {
        "w_ffn1_gate": f(inp["w_ffn1_gate"]), "w_ffn1_up": f(inp["w_ffn1_up"]), "w_ffn1_down": tile_down(inp["w_ffn1_down"]),
        "w_ffn2_gate": f(inp["w_ffn2_gate"]), "w_ffn2_up": f(inp["w_ffn2_up"]), "w_ffn2_down": tile_down(inp["w_ffn2_down"]),
        "w_in": np.ascontiguousarray(w_in_ext), "w_branch": np.ascontiguousarray(f(inp["w_branch"]).reshape(NL, 16, 128, 16, 128).transpose(0, 3, 2, 1, 4)),
        "w_gate": w_gate, "w_out": f(inp["w_out"]),
    }
    vecs = np.stack([f(inp["g_ffn1"])[0], f(inp["g_ffn1"])[1], f(inp["g_mix"])[0], f(inp["g_mix"])[1],
                     f(inp["g_ffn2"])[0], f(inp["g_ffn2"])[1], f(inp["g_final"])])
    shared["gvec"] = np.ascontiguousarray(vecs.reshape(7, 16, 128).transpose(2, 0, 1))
    gsm = np.stack([f(inp["pool_scale"]).reshape(NL, 4, 128), f(inp["g_ret"]).reshape(NL, 4, 128)], axis=1)
    shared["gsm"] = np.ascontiguousarray(gsm.transpose(3, 0, 1, 2))
    shared["ggm"] = f(inp["g_gmlp"])
    bsp = f(inp["b_spatial"])
    shared["bsp"] = np.ascontiguousarray(bsp.reshape(NL, 512))
    shared["bsp_s"] = np.ascontiguousarray(np.tile(bsp[:, :, None, :4], (1, 1, 4, 1)).reshape(NL, 4 * TS))
    wsp = f(inp["w_spatial"])
    tril = np.tril(np.ones((128, 128), bool))
    wtr = np.where(tril[None, None], wsp, np.float32(0))
    shared["wsT"] = np.ascontiguousarray(wtr.transpose(0, 3, 1, 2))
    wss = np.zeros((NL, TS, 4, TS), np.float32)
    for q in range(4):
        wss[:, q * 4:(q + 1) * 4, :, q * 4:(q + 1) * 4] = wtr[:, :, :4, :4].transpose(0, 3, 1, 2)
    shared["wsT_s"] = wss
    shared["wpool"] = np.ascontiguousarray(f(inp["w_pool"]).transpose(0, 2, 1, 3))
    shared["tbl"] = host_tables(f(inp["rel_bias"]))
    cst = host_consts()
    for k, v in cst.items():
        if isinstance(v, np.ndarray):
            shared[k] = v
    xp = f(inp["x_prompt"]); xs = f(inp["x_sample"])
    caches = [f(inp["cache_attn_kv_w128"]), f(inp["cache_attn_kv_w512"]), f(inp["cache_attn_kv_w2048"])]
    spool = f(inp["state_pool"]); sret = f(inp["state_ret"])
    maps = []
    for c in range(8):
        m = dict(shared)
        m["xp"] = xp[c % 4]
        m["xs"] = np.ascontiguousarray(xs[4 * c:4 * c + 4].reshape(TS, D))
        for g in range(3):
            m["ckv%d" % g] = np.ascontiguousarray(caches[g][:, 4 * c:4 * c + 4])
        m["spool"] = np.ascontiguousarray(spool[:, 4 * c:4 * c + 4])
        m["sret"] = np.ascontiguousarray(sret[:, 4 * c:4 * c + 4])
        maps.append(m)
    return maps


def kernel(**inputs):
    global _PROG
    if _PROG is None:
        _PROG = Prog()
    pr = _PROG
    maps = _layout_inputs(inputs)
    for m in maps:
        assert set(m.keys()) == set(pr.input_specs.keys()), (set(m.keys()) ^ set(pr.input_specs.keys()))
    res = run_bass_kernel_spmd(pr.nc, maps, core_ids=list(range(8))).results
    B = 4
    y_p = np.stack([res[b]["y_p"] for b in range(B)])
    y_s = np.concatenate([res[c]["y_s"].reshape(4, 4, D) for c in range(8)], axis=0)
    kvp = [np.stack([res[b]["kv_p%d" % g] for b in range(B)], axis=1) for g in range(3)]
    kvs = [np.concatenate([res[c]["kv_s%d" % g] for c in range(8)], axis=1) for g in range(3)]
    pool_p = np.stack([res[b]["pool_p"] for b in range(B)], axis=1)
    pool_s = np.concatenate([res[c]["pool_s"] for c in range(8)], axis=1)
    ret_p = np.stack([res[b]["ret_p"] for b in range(B)], axis=1)
    ret_s = np.concatenate([res[c]["ret_s"] for c in range(8)], axis=1)
    gv_s = np.concatenate([res[c]["gv_s"] for c in range(8)], axis=1)
    outs = (y_p, y_s, kvp[0], kvp[1], kvp[2], kvs[0], kvs[1], kvs[2], pool_p, pool_s, ret_p, ret_s, gv_s)
    return tuple(np.ascontiguousarray(o, dtype=np.float32) for o in outs)
```
